# Optimizing a Trainium2 kernel written in Bass

```python
import jax
import jax.numpy as jnp
from jax import lax
import numpy as np

D_MODEL = 2048
BATCH = 8
SEQ = 2048
DEPTH = 2

CTX_LEN = 256
GRID_W = 64
N_MIXERS = 2
EPS = 1e-6
N_MOD = 6
RG_WIDTH = D_MODEL
RG_HEADS = 16
RG_HEAD_DIM = RG_WIDTH // RG_HEADS
RG_CONV = 4
RG_CONV_LEFT = 2
RG_C = 8.0
RG_A_MIN = 0.9
RG_A_MAX = 0.999
HG_HEADS = 16
HG_DK = D_MODEL // HG_HEADS
HG_DV = D_MODEL // HG_HEADS
HG_CHUNK = 32
D_FF = ((8 * D_MODEL + 3 * 256 - 1) // (3 * 256)) * 256
N_RG = (DEPTH + N_MIXERS - 1) // N_MIXERS
N_HG = DEPTH // N_MIXERS

kernel_name = 'hybrid_rglru_hgrn2_flow_block'


def _rms(x, g):
    xf = x.astype(jnp.float32)
    y = xf * lax.rsqrt(jnp.mean(xf * xf, axis=-1, keepdims=True) + EPS)
    return y.astype(x.dtype) * g


def _ada(cvec, w, b):
    m = jax.nn.silu(cvec) @ w + b
    return [t[:, None, :] for t in jnp.split(m, N_MOD, axis=-1)]


def _modulate(x, g, shift, scale):
    return _rms(x, g) * (1.0 + scale) + shift


def _swiglu(h, w_in, w_out):
    gate, up = jnp.split(h @ w_in, 2, axis=-1)
    return (jax.nn.silu(gate) * up) @ w_out


def _to_col_major(h, rows):
    bn, l, d = h.shape
    return h.reshape(bn, rows, GRID_W, d).transpose(0, 2, 1, 3).reshape(bn, l, d)


def _from_col_major(h, rows):
    bn, l, d = h.shape
    return h.reshape(bn, GRID_W, rows, d).transpose(0, 2, 1, 3).reshape(bn, l, d)


def _centred_conv(x, w, b):
    l = x.shape[1]
    xp = jnp.pad(x, ((0, 0), (RG_CONV_LEFT, RG_CONV - 1 - RG_CONV_LEFT), (0, 0)))
    y = b
    for k in range(RG_CONV):
        y = y + w[k] * xp[:, k:k + l]
    return y


def _rglru_coeffs(xc, w_a, b_a, w_i, b_i, lam):
    bn, l, wd = xc.shape
    xh = xc.reshape(bn, l, RG_HEADS, RG_HEAD_DIM)
    r = jax.nn.sigmoid(jnp.einsum('blhd,hde->blhe', xh, w_a) + b_a).reshape(bn, l, wd).astype(jnp.float32)
    ig = jax.nn.sigmoid(jnp.einsum('blhd,hde->blhe', xh, w_i) + b_i).reshape(bn, l, wd).astype(jnp.float32)
    log_a = -RG_C * r * jax.nn.softplus(-lam.astype(jnp.float32))
    a = jnp.exp(log_a)
    b = jnp.sqrt(-jnp.expm1(2.0 * log_a)) * ig * xc.astype(jnp.float32)
    return a, b


def _linear_scan(a, b, h0):
    def comb(lft, rgt):
        return (lft[0] * rgt[0], rgt[0] * lft[1] + rgt[1])
    a_cum, b_cum = lax.associative_scan(comb, (a, b), axis=1)
    h = a_cum * h0[:, None, :] + b_cum
    return h, h[:, -1]


def _rglru_dir(xc_ctx, xc_lat, w_a, b_a, w_i, b_i, lam, reverse):
    flip = (lambda t: jnp.flip(t, axis=1)) if reverse else (lambda t: t)
    h0 = jnp.zeros((xc_ctx.shape[0], RG_WIDTH), jnp.float32)
    a_c, b_c = _rglru_coeffs(flip(xc_ctx), w_a, b_a, w_i, b_i, lam)
    h_c, s_c = _linear_scan(a_c, b_c, h0)
    a_l, b_l = _rglru_coeffs(flip(xc_lat), w_a, b_a, w_i, b_i, lam)
    h_l, _ = _linear_scan(a_l, b_l, s_c)
    return flip(h_c), flip(h_l)


def _rglru_mixer(h_ctx, h_lat, w_in, conv_w, conv_b, w_a, b_a, w_i, b_i, lam, w_out, need_ctx):
    xb_c, gb_c = jnp.split(h_ctx @ w_in, 2, axis=-1)
    xb_l, gb_l = jnp.split(h_lat @ w_in, 2, axis=-1)
    xc_c = _centred_conv(xb_c, conv_w, conv_b)
    xc_l = _centred_conv(xb_l, conv_w, conv_b)
    hcf, hlf = _rglru_dir(xc_c, xc_l, w_a[0], b_a[0], w_i[0], b_i[0], lam[0], False)
    hcb, hlb = _rglru_dir(xc_c, xc_l, w_a[1], b_a[1], w_i[1], b_i[1], lam[1], True)
    y_l = ((hlf + hlb).astype(h_lat.dtype) * jax.nn.gelu(gb_l)) @ w_out
    y_c = ((hcf + hcb).astype(h_ctx.dtype) * jax.nn.gelu(gb_c)) @ w_out if need_ctx else None
    return y_c, y_l


def _hgrn2_chunk_scan(q, k, v, log_f, s0):
    bn, l, h, _ = q.shape
    n = l // HG_CHUNK

    def chunks(t):
        return t.reshape(bn, n, HG_CHUNK, h, t.shape[-1]).transpose(1, 0, 3, 2, 4)

    mask = jnp.tril(jnp.ones((HG_CHUNK, HG_CHUNK), dtype=bool))

    def step(s, inp):
        qc, kc, vc, gc = inp
        b = jnp.cumsum(gc, axis=2)
        o_inter = jnp.einsum('bhjd,bhde->bhje', qc * jnp.exp(b), s)
        diff = jnp.where(mask[:, :, None], b[:, :, :, None, :] - b[:, :, None, :, :], -jnp.inf)
        att = jnp.einsum('bhjsd,bhsd->bhjs', qc[:, :, :, None, :] * jnp.exp(diff), kc)
        o_intra = jnp.einsum('bhjs,bhse->bhje', att, vc)
        b_last = b[:, :, -1:, :]
        s_new = jnp.exp(b_last[:, :, 0, :, None]) * s + jnp.einsum('bhsd,bhse->bhde', kc * jnp.exp(b_last - b), vc)
        return s_new, o_inter + o_intra

    s_fin, o = lax.scan(step, s0, (chunks(q), chunks(k), chunks(v), chunks(log_f)))
    o = o.transpose(1, 0, 3, 2, 4).reshape(bn, l, h, v.shape[-1])
    return o, s_fin


def _hgrn2_feats(h, w_in, lb):
    bn, l, _ = h.shape
    q, f_fwd, f_bwd, v, g = jnp.split(h @ w_in, 5, axis=-1)
    heads = lambda t: t.reshape(bn, l, HG_HEADS, -1).astype(jnp.float32)
    q = heads(q) * (HG_DK ** -0.5)
    gates = []
    for d, fz in enumerate((f_fwd, f_bwd)):
        z = heads(fz)
        lbd = lb[d]
        log_f = jnp.logaddexp(jnp.log(lbd), jnp.log1p(-lbd) + jax.nn.log_sigmoid(z))
        k = (1.0 - lbd) * jax.nn.sigmoid(-z)
        gates.append((k, log_f))
    return q, heads(v), heads(g), gates


def _hgrn2_dir(q_c, k_c, v_c, lf_c, q_l, k_l, v_l, lf_l, reverse):
    flip = (lambda t: jnp.flip(t, axis=1)) if reverse else (lambda t: t)
    s0 = jnp.zeros((q_c.shape[0], HG_HEADS, HG_DK, HG_DV), jnp.float32)
    o_c, s_c = _hgrn2_chunk_scan(flip(q_c), flip(k_c), flip(v_c), flip(lf_c), s0)
    o_l, _ = _hgrn2_chunk_scan(flip(q_l), flip(k_l), flip(v_l), flip(lf_l), s_c)
    return flip(o_c), flip(o_l)


def _hgrn2_out(o, g, norm_w, w_out, dtype):
    bn, l = o.shape[:2]
    o = o * lax.rsqrt(jnp.mean(o * o, axis=-1, keepdims=True) + EPS) * norm_w * jax.nn.silu(g)
    return o.reshape(bn, l, HG_HEADS * HG_DV).astype(dtype) @ w_out


def _hgrn2_mixer(h_ctx, h_lat, w_in, lb, norm_w, w_out, need_ctx):
    qc, vc, gc, gates_c = _hgrn2_feats(h_ctx, w_in, lb)
    ql, vl, gl, gates_l = _hgrn2_feats(h_lat, w_in, lb)
    ocf, olf = _hgrn2_dir(qc, gates_c[0][0], vc, gates_c[0][1], ql, gates_l[0][0], vl, gates_l[0][1], False)
    ocb, olb = _hgrn2_dir(qc, gates_c[1][0], vc, gates_c[1][1], ql, gates_l[1][0], vl, gates_l[1][1], True)
    y_l = _hgrn2_out(olf + olb, gl, norm_w, w_out, h_lat.dtype)
    y_c = _hgrn2_out(ocf + ocb, gc, norm_w, w_out, h_ctx.dtype) if need_ctx else None
    return y_c, y_l


def setup_inputs(seed: int = 0) -> dict:
    key = jax.random.key(seed)
    ks = jax.random.split(key, 32)
    f32 = jnp.float32
    nrm = lambda k, shape, s: jax.random.normal(k, shape, f32) * s
    D, W, F = D_MODEL, RG_WIDTH, D_FF
    u = jax.random.uniform(ks[14], (N_RG, 2, W), f32, RG_A_MIN, RG_A_MAX)
    s = u ** (1.0 / RG_C)
    return {
        'x': nrm(ks[0], (BATCH, SEQ, D), 1.0),
        'c': nrm(ks[1], (BATCH, D), 1.0),
        'ctx': nrm(ks[2], (BATCH, CTX_LEN, D), 1.0),
        'c_ctx': nrm(ks[3], (D,), 1.0),
        'w_ada': nrm(ks[4], (DEPTH, D, N_MOD * D), 0.5 * D ** -0.5),
        'b_ada': nrm(ks[5], (DEPTH, N_MOD * D), 0.01),
        'g_mix': 1.0 + nrm(ks[6], (DEPTH, D), 0.02),
        'g_ffn': 1.0 + nrm(ks[7], (DEPTH, D), 0.02),
        'g_final': 1.0 + nrm(ks[8], (D,), 0.02),
        'w_ffn_in': nrm(ks[9], (DEPTH, D, 2 * F), D ** -0.5),
        'w_ffn_out': nrm(ks[10], (DEPTH, F, D), F ** -0.5),
        'rg_w_in': nrm(ks[11], (N_RG, D, 2 * W), D ** -0.5),
        'rg_conv_w': nrm(ks[12], (N_RG, RG_CONV, W), RG_CONV ** -0.5),
        'rg_conv_b': nrm(ks[13], (N_RG, W), 0.01),
        'rg_w_a': nrm(ks[15], (N_RG, 2, RG_HEADS, RG_HEAD_DIM, RG_HEAD_DIM), RG_HEAD_DIM ** -0.5),
        'rg_b_a': nrm(ks[16], (N_RG, 2, RG_HEADS, RG_HEAD_DIM), 0.01),
        'rg_w_i': nrm(ks[17], (N_RG, 2, RG_HEADS, RG_HEAD_DIM, RG_HEAD_DIM), RG_HEAD_DIM ** -0.5),
        'rg_b_i': nrm(ks[18], (N_RG, 2, RG_HEADS, RG_HEAD_DIM), 0.01),
        'rg_lam': jnp.log(s) - jnp.log1p(-s),
        'rg_w_out': nrm(ks[19], (N_RG, W, D), W ** -0.5),
        'hg_w_in': nrm(ks[20], (N_HG, D, 5 * D), D ** -0.5),
        'hg_lb': nrm(ks[21], (DEPTH, 2, HG_HEADS * HG_DK), 0.1),
        'hg_norm': 1.0 + nrm(ks[22], (N_HG, HG_DV), 0.02),
        'hg_w_out': nrm(ks[23], (N_HG, HG_HEADS * HG_DV, D), D ** -0.5),
    }


def reference(x, c, ctx, c_ctx, w_ada, b_ada, g_mix, g_ffn, g_final, w_ffn_in, w_ffn_out,
              rg_w_in, rg_conv_w, rg_conv_b, rg_w_a, rg_b_a, rg_w_i, rg_b_i, rg_lam, rg_w_out,
              hg_w_in, hg_lb, hg_norm, hg_w_out):
    rows = x.shape[1] // GRID_W
    lb_all = jnp.cumsum(jax.nn.softmax(hg_lb.astype(jnp.float32), axis=0), axis=0)
    xl, xc = x, ctx
    for i in range(DEPTH):
        last = i == DEPTH - 1
        sh1, sc1, ga1, sh2, sc2, ga2 = _ada(c, w_ada[i], b_ada[i])
        csh1, csc1, cga1, csh2, csc2, cga2 = _ada(c_ctx[None, :], w_ada[i], b_ada[i])
        hl = _modulate(xl, g_mix[i], sh1, sc1)
        hc = _modulate(xc, g_mix[i], csh1, csc1)
        j = i // N_MIXERS
        if i % N_MIXERS == 0:
            yc, yl = _rglru_mixer(hc, hl, rg_w_in[j], rg_conv_w[j], rg_conv_b[j], rg_w_a[j], rg_b_a[j],
                                  rg_w_i[j], rg_b_i[j], rg_lam[j], rg_w_out[j], not last)
        else:
            lb = (lb_all[i] - lb_all[0]).reshape(2, HG_HEADS, HG_DK)
            yc, yl = _hgrn2_mixer(hc, _to_col_major(hl, rows), hg_w_in[j], lb, hg_norm[j], hg_w_out[j], not last)
            yl = _from_col_major(yl, rows)
        xl = xl + ga1 * yl
        xl = xl + ga2 * _swiglu(_modulate(xl, g_ffn[i], sh2, sc2), w_ffn_in[i], w_ffn_out[i])
        if not last:
            xc = xc + cga1 * yc
            xc = xc + cga2 * _swiglu(_modulate(xc, g_ffn[i], csh2, csc2), w_ffn_in[i], w_ffn_out[i])
    return _rms(xl, g_final)
```

```python
import contextlib
import itertools
import numpy as np
import concourse.bass as bass
import concourse.mybir as mybir
from concourse.bass_utils import run_bass_kernel_spmd

F32 = mybir.dt.float32
BF16 = mybir.dt.bfloat16
AF = mybir.ActivationFunctionType
ALU = mybir.AluOpType
AX = mybir.AxisListType

D = 2048
KC = 16
TC = 256
TL = 2048
T = TC + TL
FF = 5632
FC = 44
EPS = 1e-6
STREAMS = ["pe", "act", "dve", "pool", "sp"]


class Buf:
    __slots__ = ("name", "w", "rc", "rd")

    def __init__(self, name=""):
        self.name = name
        self.w = None
        self.rc = {}
        self.rd = []


class Op:
    __slots__ = ("st", "fn", "deps", "dma", "sig", "sem", "val", "pre")

    def __init__(self, st, fn, dma):
        self.st = st
        self.fn = fn
        self.dma = dma
        self.deps = []
        self.sig = False
        self.sem = None
        self.val = 0
        self.pre = None


class Sched:
    NDMASEM = 8

    def __init__(self, nc):
        self.nc = nc
        self.ops = {s: [] for s in STREAMS}
        self.pending = {s: [] for s in STREAMS}
        self.dma_since = {s: [] for s in STREAMS}
        self.plan = False

    def add(self, st, fn, reads=(), writes=(), dma=False):
        if self.plan:
            return None
        op = Op(st, fn, dma)
        deps = []

        def dep(d, hard):
            if d is None or d is op:
                return
            if (not hard) and (not d.dma) and (not dma) and d.st == st:
                return
            deps.append(d)

        for b in reads:
            dep(b.w, True)
        for b in writes:
            dep(b.w, False)
            for r in b.rc.values():
                dep(r, False)
            for r in b.rd:
                dep(r, False)
        for b in reads:
            if dma:
                b.rd.append(op)
            else:
                b.rc[st] = op
        for b in writes:
            b.w = op
            b.rc = {}
            b.rd = []
        if self.pending[st]:
            deps.extend(self.pending[st])
            self.pending[st] = []
        op.deps = deps
        for d in deps:
            d.sig = True
        if dma:
            op.sig = True
            self.dma_since[st].append(op)
        self.ops[st].append(op)
        return op

    def barrier(self, streams=("pe", "act", "dve", "sp")):
        if self.plan:
            return
        fr = {}
        for s in streams:
            f = list(self.dma_since[s])
            self.dma_since[s] = []
            for op in reversed(self.ops[s]):
                if not op.dma:
                    f.append(op)
                    break
            fr[s] = f
        for s in streams:
            for s2 in streams:
                if s2 != s:
                    self.pending[s].extend(fr[s2])
                else:
                    self.pending[s].extend([o for o in fr[s2] if o.dma])

    def emit(self, final_waits=()):
        nc = self.nc
        es = contextlib.ExitStack()
        csem = {s: es.enter_context(nc.semaphore("c_" + s)) for s in STREAMS}
        dsem = {s: [es.enter_context(nc.semaphore("d_%s%d" % (s, i))) for i in range(self.NDMASEM)]
                for s in STREAMS}
        for s in STREAMS:
            cnt = 0
            dcount = [0] * self.NDMASEM
            di = 0
            for op in self.ops[s]:
                if op.dma:
                    k = di % self.NDMASEM
                    di += 1
                    if dcount[k] > 0:
                        op.pre = (dsem[s][k], dcount[k] * 16)
                    dcount[k] += 1
                    op.sem = dsem[s][k]
                    op.val = dcount[k] * 16
                elif op.sig:
                    cnt += 1
                    op.sem = csem[s]
                    op.val = cnt
        final = {}
        for op in final_waits:
            if op is not None:
                final.setdefault(op.st, []).append(op)
        block = es.enter_context(nc.Block())

        def run_stream(s):
            def body(e):
                known = {}

                def wait(sem, val):
                    key = id(sem)
                    if known.get(key, 0) >= val:
                        return
                    known[key] = val
                    e.wait_ge(sem, val)

                for op in self.ops[s]:
                    if op.pre is not None:
                        wait(*op.pre)
                    for d in op.deps:
                        wait(d.sem, d.val)
                    ins = op.fn(e)
                    if op.sig:
                        ins.then_inc(op.sem, 16 if op.dma else 1)
                for op in final.get(s, []):
                    wait(op.sem, op.val)
            return body

        block.tensor(run_stream("pe"))
        block.scalar(run_stream("act"))
        block.vector(run_stream("dve"))
        block.gpsimd(run_stream("pool"))
        block.sync(run_stream("sp"))
        es.close()


PV_SEGS = [("c", 16), ("cctx", 16), ("bada", 192), ("gmix", 32), ("gffn", 32), ("gfin", 16),
           ("convw", 64), ("convb", 16), ("ba", 32), ("bi", 32), ("lam", 32), ("hglb", 64), ("hgnorm", 1)]
PV_OFF = {}
_o = 0
for _n, _s in PV_SEGS:
    PV_OFF[_n] = _o
    _o += _s
NPV = _o
CST_SEGS = [("ident", 128), ("trif", 128), ("trib", 128), ("ones", 128), ("smask", T + 4)]
CST_OFF = {}
_o = 0
for _n, _s in CST_SEGS:
    CST_OFF[_n] = _o
    _o += _s
NCST = _o


def _fm(v):
    v = np.asarray(v, np.float32)
    lead = v.shape[:-1]
    k = v.shape[-1] // 128
    v = v.reshape(lead + (k, 128))
    v = np.moveaxis(v, -1, 0)
    return np.ascontiguousarray(v).reshape(128, -1)


def _consts():
    c = np.zeros((128, NCST), np.float32)
    i = np.arange(128)
    c[:, CST_OFF["ident"]:CST_OFF["ident"] + 128] = np.eye(128, dtype=np.float32)
    c[:, CST_OFF["trif"]:CST_OFF["trif"] + 128] = (i[None, :] >= i[:, None])
    c[:, CST_OFF["trib"]:CST_OFF["trib"] + 128] = (i[None, :] <= i[:, None])
    c[:, CST_OFF["ones"]:CST_OFF["ones"] + 128] = 1.0
    u = np.arange(T + 4)
    c[:, CST_OFF["smask"]:CST_OFF["smask"] + T + 4] = (u % 128 != 0)[None, :]
    return c


def build(debug=(), stop_after=None):
    nc = bass.Bass("TRN2", target_bir_lowering=False)
    S = Sched(nc)

    def din(name, shape, dt=F32):
        return nc.dram_tensor(name, list(shape), dt, kind="ExternalInput").ap()

    x_d = din("x", [TL, D])
    ctx_d = din("ctx", [TC, D])
    pv_d = din("pv", [128, NPV])
    cst_d = din("cst", [128, NCST])
    w_ada = din("w_ada", [2, D, 6 * D])
    w_ffn_in = din("w_ffn_in", [2, D, 2 * FF])
    w_ffn_out = din("w_ffn_out", [2, FF, D])
    rg_w_in = din("rg_w_in", [D, 2 * D])
    rg_w_a = din("rg_w_a", [2, 16, 128, 128])
    rg_w_i = din("rg_w_i", [2, 16, 128, 128])
    rg_w_out = din("rg_w_out", [D, D])
    hg_w_in = din("hg_w_in", [D, 5 * D])
    hg_w_out = din("hg_w_out", [D, D])
    out_d = nc.dram_tensor("out", [TL, D], F32, kind="ExternalOutput").ap()
    xT_d = nc.dram_tensor("xT_s", [KC, 128, T], F32, kind="Internal").ap()
    uT_d = nc.dram_tensor("uT_s", [KC, 128, T], BF16, kind="Internal").ap()
    F_d = nc.dram_tensor("F_s", [80, 128, T], F32, kind="Internal").ap()
    Fv_d = nc.dram_tensor("Fv_s", [16, 128, T], BF16, kind="Internal").ap()
    hT_d = nc.dram_tensor("hT_s", [16, 128, 1152], BF16, kind="Internal").ap()
    hT1_d = nc.dram_tensor("hT1_s", [16, 128, T], BF16, kind="Internal").ap()
    dbg_out = {}
    for name in debug:
        if name.startswith("xT"):
            dbg_out[name] = nc.dram_tensor("dbg_" + name, [KC, 128, T], F32, kind="ExternalOutput").ap()
        elif name.startswith("F"):
            dbg_out[name] = nc.dram_tensor("dbg_" + name, [80, 128, T], F32, kind="ExternalOutput").ap()
        elif name.startswith("uT"):
            dbg_out[name] = nc.dram_tensor("dbg_" + name, [KC, 128, T], BF16, kind="ExternalOutput").ap()
        elif name == "small":
            dbg_out[name] = nc.dram_tensor("dbg_small", [128, 1024], F32, kind="ExternalOutput").ap()
    finals = []

    pv = nc.alloc_sbuf_tensor("pvsb", [128, NPV], F32)
    cst = nc.alloc_sbuf_tensor("cstsb", [128, NCST - (T + 4)], F32)
    modv = nc.alloc_sbuf_tensor("modv", [128, 2, 96, 2], F32)
    gm = nc.alloc_sbuf_tensor("gm", [128, 2, 2, 16, 2], F32)
    gfb = nc.alloc_sbuf_tensor("gfb", [128, 16], F32)
    c1 = nc.alloc_sbuf_tensor("c1", [128, 32], F32)
    lbv = nc.alloc_sbuf_tensor("lbv", [128, 3, 32], F32)
    scb = nc.alloc_sbuf_tensor("scb", [128, 16, 2], BF16)
    cbf = nc.alloc_sbuf_tensor("cbf", [128, 4, 128], BF16)
    small = nc.alloc_sbuf_tensor("smallt", [128, 256], F32)
    NSLOT = 3
    SLOTE = 8192
    wslot = nc.alloc_sbuf_tensor("wslot", [128, NSLOT, SLOTE], BF16)
    AW = 38400
    arena = nc.alloc_sbuf_tensor("arena", [128, AW], F32)
    pbank = [nc.alloc_psum_tensor("pb%d" % i, [128, 512], F32) for i in range(8)]
    Pb = [Buf("pb%d" % i) for i in range(8)]
    Bpv, Bcst, Bmod, Bgm, Bc1, Blb, Bscb, Bcbf, Bsmall = [Buf(n) for n in
                                                         ("pv", "cst", "mod", "gm", "c1", "lb", "scb", "cbf", "small")]

    def pvs(name, j0=0, n=None):
        o = PV_OFF[name] + j0
        if n is None:
            n = dict(PV_SEGS)[name] - j0
        return pv[:, o:o + n]

    def col(ap2d, j):
        return ap2d[:, j:j + 1]

    ident_f = cst[:, CST_OFF["ident"]:CST_OFF["ident"] + 128]
    ident_b = cbf[:, 0, :]
    trif_b = cbf[:, 1, :]
    trib_b = cbf[:, 2, :]
    ones_b = cbf[:, 3, :]

    def fap(ap2d, off, dims):
        return bass.AP(ap2d.tensor, ap2d.offset + off, [[ap2d.ap[0][0], ap2d.ap[0][1]]] + [list(d) for d in dims])

    class Arena:
        def __init__(self):
            self.off = 0

        def reset(self):
            self.off = 0

        def f32(self, n):
            a = arena[:, self.off:self.off + n]
            self.off += n
            assert self.off <= AW, ("arena overflow", self.off)
            return a

        def bf16(self, n):
            w = (n + 1) // 2
            a = arena[:, self.off:self.off + w].bitcast(BF16)
            self.off += w
            assert self.off <= AW, ("arena overflow", self.off)
            return a[:, 0:n]

    AR = Arena()

    def interleave(*gens):
        gens = list(gens)
        while gens:
            for g_ in list(gens):
                try:
                    next(g_)
                except StopIteration:
                    gens.remove(g_)

    class WS:
        def __init__(self):
            self.specs = []
            self.i = 0
            self.issued = 0
            self.bufs = [Buf("ws%d" % k) for k in range(NSLOT)]

        def _issue(self, idx):
            k = idx % NSLOT
            for (off, shape, src) in self.specs[idx]:
                n = int(np.prod(shape))
                dst = wslot[:, k, off:off + n]
                if len(shape) == 2:
                    dst = dst.rearrange("p (a b) -> p a b", a=shape[0])
                S.add("pool", lambda e, dst=dst, src=src: e.dma_start(out=dst, in_=src),
                      writes=[self.bufs[k]], dma=True)

        def next(self, spec, ahead=NSLOT - 1):
            if S.plan:
                self.specs.append(spec)
                return 0, None
            idx = self.i
            self.i += 1
            while self.issued < min(len(self.specs), idx + 1 + ahead):
                self._issue(self.issued)
                self.issued += 1
            return idx % NSLOT, self.bufs[idx % NSLOT]

    ws = WS()
    pbi = [0]

    def nextbank():
        k = pbi[0] % 7
        pbi[0] += 1
        return pbank[k], Pb[k]

    def wrows(wap, c0, ncols):
        return wap.rearrange("(kc p) n -> p kc n", p=128)[:, :, c0:c0 + ncols]

    def dbg_dump(name, src_d, B=None):
        if name in dbg_out:
            finals.append(S.add("sp", lambda e: e.dma_start(out=dbg_out[name], in_=src_d), dma=True))

    BLK_ALL = [(0, 256, 1), (256, 512, 0), (768, 512, 0), (1280, 512, 0), (1792, 512, 0)]
    BLK_LAT = BLK_ALL[1:]

    epsc = small[:, 255:256]
    onec = small[:, 254:255]
    Bsmall2 = Buf("constcols")

    def program():
        AR.reset()
        S.add("dve", lambda e: e.memset(epsc, EPS), writes=[Bsmall2])
        S.add("dve", lambda e: e.memset(onec, 1.0), writes=[Bsmall2])
        S.add("sp", lambda e: e.dma_start(out=pv[:], in_=pv_d), writes=[Bpv], dma=True)
        S.add("sp", lambda e: e.dma_start(out=cst[:], in_=cst_d[:, 0:NCST - (T + 4)]), writes=[Bcst], dma=True)
        for k, nm in enumerate(("ident", "trif", "trib", "ones")):
            S.add("dve", lambda e, k=k, nm=nm: e.tensor_copy(out=cbf[:, k, :],
                                                            in_=cst[:, CST_OFF[nm]:CST_OFF[nm] + 128]),
                  reads=[Bcst], writes=[Bcbf])
        S.add("act", lambda e: e.activation(out=scb[:, :, 0], in_=pvs("c"), func=AF.Silu), reads=[Bpv], writes=[Bscb])
        S.add("act", lambda e: e.activation(out=scb[:, :, 1], in_=pvs("cctx"), func=AF.Silu), reads=[Bpv], writes=[Bscb])
        sm = small
        S.add("act", lambda e: e.activation(out=sm[:, 0:32], in_=pvs("lam"), func=AF.Exp, scale=-1.0),
              reads=[Bpv], writes=[Bsmall])
        S.add("dve", lambda e: e.tensor_scalar(out=sm[:, 32:64], in0=sm[:, 0:32], scalar1=1.0, scalar2=None, op0=ALU.add),
              reads=[Bsmall], writes=[Bsmall])
        S.add("act", lambda e: e.activation(out=sm[:, 64:96], in_=sm[:, 32:64], func=AF.Ln), reads=[Bsmall], writes=[Bsmall])
        S.add("dve", lambda e: e.tensor_scalar(out=sm[:, 96:128], in0=sm[:, 32:64], scalar1=-1.0, scalar2=None, op0=ALU.add),
              reads=[Bsmall], writes=[Bsmall])
        S.add("dve", lambda e: e.reciprocal(out=sm[:, 96:128], in_=sm[:, 96:128]), reads=[Bsmall], writes=[Bsmall])
        S.add("dve", lambda e: e.tensor_tensor(out=sm[:, 96:128], in0=sm[:, 96:128], in1=sm[:, 0:32], op=ALU.mult),
              reads=[Bsmall], writes=[Bsmall])
        S.add("dve", lambda e: e.scalar_tensor_tensor(out=c1[:], in0=sm[:, 64:96], scalar=-8.0, in1=sm[:, 96:128],
                                                       op0=ALU.mult, op1=ALU.mult),
              reads=[Bsmall], writes=[Bc1])
        S.add("dve", lambda e: e.tensor_tensor(out=sm[:, 128:160], in0=pvs("hglb", 32, 32), in1=pvs("hglb", 0, 32),
                                               op=ALU.subtract), reads=[Bpv], writes=[Bsmall])
        S.add("act", lambda e: e.activation(out=lbv[:, 0, :], in_=sm[:, 128:160], func=AF.Sigmoid),
              reads=[Bsmall], writes=[Blb])
        S.add("dve", lambda e: e.tensor_scalar(out=lbv[:, 1, :], in0=lbv[:, 0, :], scalar1=-1.0, scalar2=1.0,
                                               op0=ALU.mult, op1=ALU.add), reads=[Blb], writes=[Blb])
        S.add("dve", lambda e: e.tensor_scalar(out=lbv[:, 2, :], in0=lbv[:, 0, :], scalar1=-1.0, scalar2=None,
                                               op0=ALU.add), reads=[Blb], writes=[Blb])

        def ada_gen(li, g0=0, g1=24):
            pt, pB = pbank[7], Pb[7]
            for g in range(g0, g1):
                k, wb = ws.next([(0, (16, 512), wrows(w_ada[li], g * 512, 512))])
                wv = wslot[:, k, 0:8192].rearrange("p (a b) -> p a b", a=16)
                for o4 in range(4):
                    oc = g * 4 + o4
                    for kc in range(16):
                        S.add("pe", lambda e, wv=wv, o4=o4, kc=kc, oc=oc, pt=pt: e.matmul(
                            pt[:, oc * 2:oc * 2 + 2], wv[:, kc, o4 * 128:(o4 + 1) * 128], scb[:, kc, :],
                            start=(kc == 0), stop=(kc == 15)),
                            reads=[wb, Bscb], writes=[pB])
                yield
            ca, cb = g0 * 4, g1 * 4
            badav = pvs("bada", li * 96 + ca, cb - ca)
            S.add("dve", lambda e, li=li, pt=pt, badav=badav: e.tensor_tensor(
                out=modv[:, li, ca:cb, :], in0=pt[:, 2 * ca:2 * cb].rearrange("p (a b) -> p a b", b=2),
                in1=fap(badav, 0, [[1, cb - ca], [0, 2]]), op=ALU.add), reads=[pB, Bpv], writes=[Bmod])
            for which, (gname, scbase) in enumerate((("gmix", 16), ("gffn", 64))):
                if not (ca <= scbase and scbase + 16 <= cb):
                    continue
                gv = pvs(gname, li * 16, 16)
                S.add("dve", lambda e, li=li, which=which, scbase=scbase, gv=gv: e.scalar_tensor_tensor(
                    out=gm[:, li, which, :, :], in0=modv[:, li, scbase:scbase + 16, :], scalar=1.0,
                    in1=fap(gv, 0, [[1, 16], [0, 2]]), op0=ALU.add, op1=ALU.mult),
                    reads=[Bmod, Bpv], writes=[Bgm])
            yield

        ada1 = itertools.chain(ada_gen(0, 8, 24), ada_gen(1))
        if "small" in dbg_out:
            interleave(ada_gen(0))
        if "small" in dbg_out:
            S.add("dve", lambda e: e.tensor_copy(out=sm[:, 160:192], in_=c1[:]), reads=[Bc1], writes=[Bsmall])
            S.add("dve", lambda e: e.tensor_copy(out=sm[:, 192:224], in_=lbv[:, 0, :]), reads=[Blb], writes=[Bsmall])
            dstg = AR.f32(1024)
            Bst = Buf()
            S.add("dve", lambda e: e.tensor_copy(out=dstg[:, 0:256], in_=sm[:]), reads=[Bsmall], writes=[Bst])
            S.add("dve", lambda e: e.tensor_copy(out=dstg[:, 256:640], in_=modv[:].rearrange("p a b c -> p (a b c)")),
                  reads=[Bmod], writes=[Bst])
            S.add("dve", lambda e: e.tensor_copy(out=dstg[:, 640:768], in_=gm[:].rearrange("p a b c d -> p (a b c d)")),
                  reads=[Bgm], writes=[Bst])
            S.add("dve", lambda e: e.memset(dstg[:, 768:1024], 0.0), writes=[Bst])
            finals.append(S.add("sp", lambda e: e.dma_start(out=dbg_out["small"], in_=dstg), reads=[Bst], dma=True))
            S.barrier()
        if stop_after == "ada":
            return

        AR.reset()
        xin = [AR.f32(2048) for _ in range(2)]
        Bxin = [Buf(), Buf()]
        stg = [AR.f32(16 * 512).rearrange("p (c t) -> p c t", c=16) for _ in range(2)]
        Bstg = [Buf(), Buf()]
        ti = 0
        for bi, (t0, n, isctx) in enumerate(BLK_ALL):
            sb = bi % 2
            for sub in range(n // 128):
                k = ti % 2
                ti += 1
                tok0 = t0 + sub * 128
                src = ctx_d[tok0:tok0 + 128, :] if isctx else x_d[tok0 - TC:tok0 - TC + 128, :]
                S.add("sp", lambda e, k=k, src=src: e.dma_start(out=xin[k], in_=src), writes=[Bxin[k]], dma=True)
                for q in range(4):
                    pt, pB = nextbank()
                    for j in range(4):
                        c = q * 4 + j
                        S.add("pe", lambda e, pt=pt, j=j, c=c, k=k: e.transpose(
                            out=pt[:, j * 128:(j + 1) * 128], in_=xin[k][:, c * 128:(c + 1) * 128], identity=ident_f),
                            reads=[Bxin[k], Bcst], writes=[pB])
                    eng = "act" if q % 2 == 0 else "dve"
                    dst = stg[sb][:, q * 4:q * 4 + 4, sub * 128:(sub + 1) * 128]
                    srcp = pt[:, :].rearrange("p (a b) -> p a b", a=4)
                    if eng == "act":
                        S.add("act", lambda e, dst=dst, srcp=srcp: e.activation(out=dst, in_=srcp, func=AF.Copy),
                              reads=[pB], writes=[Bstg[sb]])
                    else:
                        S.add("dve", lambda e, dst=dst, srcp=srcp: e.tensor_copy(out=dst, in_=srcp),
                              reads=[pB], writes=[Bstg[sb]])
            S.add("sp", lambda e, sb=sb, t0=t0, n=n: e.dma_start(
                out=xT_d[:, :, t0:t0 + n].rearrange("c p t -> p c t"), in_=stg[sb][:, :, 0:n]),
                reads=[Bstg[sb]], dma=True)
        if "small" not in dbg_out:
            interleave(ada_gen(0, 0, 8))
        S.barrier()
        dbg_dump("xT0", xT_d)
        if stop_after == "p0":
            return

        def norm_phase(li, which, blocks, hT, BhT, tokbase, colmajor=False, nbuf=2, bmax=512):
            interleave(norm_gen(li, which, blocks, hT, BhT, tokbase, nbuf=nbuf, bmax=bmax))

        def norm_gen(li, which, blocks, hT, BhT, tokbase, nbuf=2, bmax=512, dram_dest=None):
            if bmax < 512:
                blocks = [(t0 + o_, min(bmax, n - o_), ic) for (t0, n, ic) in blocks for o_ in range(0, n, bmax)]
            xb_ = [AR.f32(16 * bmax).rearrange("p (c t) -> p c t", c=16) for _ in range(nbuf)]
            Bx = [Buf() for _ in range(nbuf)]
            sq = AR.bf16(16 * bmax).rearrange("p (c t) -> p c t", c=16)
            Bsq = Buf()
            rs = AR.f32(bmax)
            Brs = Buf()
            tt = [AR.f32(bmax) for _ in range(2)]
            Bt = [Buf(), Buf()]
            if dram_dest is not None:
                stg_ = [AR.bf16(16 * bmax).rearrange("p (c t) -> p c t", c=16) for _ in range(2)]
                Bstg_ = [Buf(), Buf()]
            shbase = 0 if which == 0 else 48
            return _norm_inner(li, which, blocks, hT, BhT, tokbase, nbuf, dram_dest, xb_, Bx, sq, Bsq, rs, Brs, tt, Bt,
                               stg_ if dram_dest is not None else None, Bstg_ if dram_dest is not None else None, shbase)

        def _norm_inner(li, which, blocks, hT, BhT, tokbase, nbuf, dram_dest, xb_, Bx, sq, Bsq, rs, Brs, tt, Bt,
                        stg_, Bstg_, shbase):
            for bi, (t0, n, isctx) in enumerate(blocks):
                k = bi % nbuf
                S.add("sp", lambda e, k=k, t0=t0, n=n: e.dma_start(
                    out=xb_[k][:, :, 0:n], in_=xT_d[:, :, t0:t0 + n].rearrange("c p t -> p c t")),
                    writes=[Bx[k]], dma=True)
                S.add("act", lambda e, k=k, n=n: e.activation(out=sq[:, :, 0:n], in_=xb_[k][:, :, 0:n], func=AF.Square),
                      reads=[Bx[k]], writes=[Bsq])
                if dram_dest is not None:
                    yield
                pt, pB = nextbank()
                for c in range(16):
                    S.add("pe", lambda e, pt=pt, c=c, n=n: e.matmul(pt[:, 0:n], ones_b, sq[:, c, 0:n],
                                                                    start=(c == 0), stop=(c == 15)),
                          reads=[Bsq, Bcbf], writes=[pB])
                S.add("act", lambda e, pt=pt, n=n: e.activation(out=rs[:, 0:n], in_=pt[:, 0:n], func=AF.Sqrt,
                                                               scale=1.0 / D, bias=epsc),
                      reads=[pB, Bsmall2], writes=[Brs])
                S.add("dve", lambda e, n=n: e.reciprocal(out=rs[:, 0:n], in_=rs[:, 0:n]), reads=[Brs], writes=[Brs])
                p0 = t0 - tokbase
                sk = bi % 2
                for c in range(16):
                    kk = c % 2
                    S.add("dve", lambda e, k=k, c=c, kk=kk, n=n: e.tensor_tensor(
                        out=tt[kk][:, 0:n], in0=xb_[k][:, c, 0:n], in1=rs[:, 0:n], op=ALU.mult),
                        reads=[Bx[k], Brs], writes=[Bt[kk]])
                    if dram_dest is None:
                        dst, Bd = hT[:, c, p0:p0 + n], BhT
                    else:
                        dst, Bd = stg_[sk][:, c, 0:n], Bstg_[sk]
                    srcv = tt[kk][:, 0:n]
                    S.add("act", lambda e, dst=dst, srcv=srcv, c=c, isctx=isctx: e.activation(
                        out=dst, in_=srcv, func=AF.Identity,
                        scale=gm[:, li, which, c, isctx:isctx + 1], bias=modv[:, li, shbase + c, isctx:isctx + 1]),
                        reads=[Bt[kk], Bgm, Bmod], writes=[Bd])
                if dram_dest is not None:
                    S.add("sp", lambda e, sk=sk, p0=p0, n=n: e.dma_start(
                        out=dram_dest[:, :, p0:p0 + n].rearrange("c p t -> p c t"), in_=stg_[sk][:, :, 0:n]),
                        reads=[Bstg_[sk]], dma=True)
                yield

        def permute_colmajor(hT, BhT, tokbase):
            tmp = [AR.bf16(TL) for _ in range(2)]
            Bt = [Buf(), Buf()]
            for c in range(16):
                k = c % 2
                lat = hT[:, c, TC - tokbase:T - tokbase]
                S.add("dve", lambda e, k=k, lat=lat: e.tensor_copy(out=tmp[k], in_=lat), reads=[BhT], writes=[Bt[k]])
                S.add("dve", lambda e, k=k, lat=lat: e.tensor_copy(
                    out=lat.rearrange("p (w r) -> p w r", r=32), in_=fap(tmp[k], 0, [[1, 64], [64, 32]])),
                    reads=[Bt[k]], writes=[BhT])

        def linear_to_dram(hT, BhT, blocks, tokbase, wap, nout_chunks, dest, stage_dt):
            stgf = [AR.f32(T) for _ in range(2)]
            Bs = [Buf(), Buf()]
            for g in range(nout_chunks // 4):
                k, wb = ws.next([(0, (16, 512), wrows(wap, g * 512, 512))])
                wv = wslot[:, k, 0:8192].rearrange("p (a b) -> p a b", a=16)
                for o4 in range(4):
                    oc = g * 4 + o4
                    sk = oc % 2
                    dt_ = stage_dt(oc)
                    sview = stgf[sk] if dt_ == F32 else stgf[sk].bitcast(BF16)
                    for bi, (t0, n, isctx) in enumerate(blocks):
                        pt, pB = nextbank()
                        p0 = t0 - tokbase
                        for kc in range(16):
                            S.add("pe", lambda e, pt=pt, wv=wv, kc=kc, o4=o4, p0=p0, n=n: e.matmul(
                                pt[:, 0:n], wv[:, kc, o4 * 128:(o4 + 1) * 128], hT[:, kc, p0:p0 + n],
                                start=(kc == 0), stop=(kc == 15)), reads=[wb, BhT], writes=[pB])
                        if bi % 2 == 0:
                            S.add("act", lambda e, pt=pt, sview=sview, t0=t0, n=n: e.activation(
                                out=sview[:, t0:t0 + n], in_=pt[:, 0:n], func=AF.Copy), reads=[pB], writes=[Bs[sk]])
                        else:
                            S.add("dve", lambda e, pt=pt, sview=sview, t0=t0, n=n: e.tensor_copy(
                                out=sview[:, t0:t0 + n], in_=pt[:, 0:n]), reads=[pB], writes=[Bs[sk]])
                    ta, tb = blocks[0][0], blocks[-1][0] + blocks[-1][1]
                    S.add("sp", lambda e, oc=oc, sview=sview, ta=ta, tb=tb: e.dma_start(
                        out=dest(oc)[:, ta:tb], in_=sview[:, ta:tb]), reads=[Bs[sk]], dma=True)

        def out_proj_residual(li, wap, kchunks, act, Bact, blocks, tokbase, gate_chunk_base, slot_cols):
            interleave(_opr(li, wap, kchunks, act, Bact, blocks, tokbase, gate_chunk_base, slot_cols))

        def _opr(li, wap, kchunks, act, Bact, blocks, tokbase, gate_chunk_base, slot_cols):
            ta, tb = blocks[0][0], blocks[-1][0] + blocks[-1][1]
            xo = [AR.f32(tb - ta) for _ in range(2)]
            Bxo = [Buf(), Buf()]
            return _opr_inner(li, wap, kchunks, act, Bact, blocks, tokbase, gate_chunk_base, slot_cols, ta, tb, xo, Bxo)

        def _opr_inner(li, wap, kchunks, act, Bact, blocks, tokbase, gate_chunk_base, slot_cols, ta, tb, xo, Bxo):
            per = slot_cols // 128
            for g in range(16 // per):
                k, wb = ws.next([(0, (kchunks, slot_cols), wrows(wap, g * slot_cols, slot_cols))])
                wv = wslot[:, k, 0:kchunks * slot_cols].rearrange("p (a b) -> p a b", a=kchunks)
                for o4 in range(per):
                    oc = g * per + o4
                    sk = oc % 2
                    S.add("sp", lambda e, sk=sk, oc=oc: e.dma_start(out=xo[sk][:, 0:tb - ta], in_=xT_d[oc][:, ta:tb]),
                          writes=[Bxo[sk]], dma=True)
                    for bi, (t0, n, isctx) in enumerate(blocks):
                        pt, pB = nextbank()
                        p0 = t0 - tokbase
                        for kc in range(kchunks):
                            S.add("pe", lambda e, pt=pt, wv=wv, kc=kc, o4=o4, p0=p0, n=n: e.matmul(
                                pt[:, 0:n], wv[:, kc, o4 * 128:(o4 + 1) * 128], act[:, kc, p0:p0 + n],
                                start=(kc == 0), stop=(kc == kchunks - 1)), reads=[wb, Bact], writes=[pB])
                        S.add("dve", lambda e, pt=pt, sk=sk, t0=t0, n=n, oc=oc, isctx=isctx: e.scalar_tensor_tensor(
                            out=xo[sk][:, t0 - ta:t0 - ta + n], in0=pt[:, 0:n],
                            scalar=modv[:, li, gate_chunk_base + oc, isctx:isctx + 1],
                            in1=xo[sk][:, t0 - ta:t0 - ta + n], op0=ALU.mult, op1=ALU.add),
                            reads=[pB, Bmod, Bxo[sk]], writes=[Bxo[sk]])
                    S.add("sp", lambda e, sk=sk, oc=oc: e.dma_start(out=xT_d[oc][:, ta:tb], in_=xo[sk][:, 0:tb - ta]),
                          reads=[Bxo[sk]], dma=True)
                    yield

        def ffn_phase(li, passes):
            for pi, blocks in enumerate(passes):
                nxt = passes[pi + 1] if pi + 1 < len(passes) else None
                ffn_pass(li, blocks, pi == 0, nxt)

        def ffn_pass(li, blocks, first, nxt_blocks):
            AR.reset()
            tokbase = blocks[0][0]
            tp = blocks[-1][0] + blocks[-1][1] - tokbase
            hT = AR.bf16(16 * tp).rearrange("p (c t) -> p c t", c=16)
            BhT = Buf()
            mark = AR.off
            if first:
                norm_phase(li, 1, blocks, hT, BhT, tokbase)
                S.barrier()
            else:
                S.add("sp", lambda e: e.dma_start(out=hT, in_=hT_d[:, :, 0:tp].rearrange("c p t -> p c t")),
                      writes=[BhT], dma=True)
            AR.off = mark
            A = AR.bf16(FC * tp).rearrange("p (c t) -> p c t", c=FC)
            BA = Buf()
            sg = [AR.f32(512) for _ in range(2)]
            Bsg = [Buf(), Buf()]
            cnt = 0
            for u in range(FC // 2):
                k, wb = ws.next([(0, (16, 256), wrows(w_ffn_in[li], u * 256, 256)),
                                 (4096, (16, 256), wrows(w_ffn_in[li], FF + u * 256, 256))])
                wg = wslot[:, k, 0:4096].rearrange("p (a b) -> p a b", a=16)
                wu = wslot[:, k, 4096:8192].rearrange("p (a b) -> p a b", a=16)
                for jj in range(2):
                    j = u * 2 + jj
                    for (t0, n, isctx) in blocks:
                        p0 = t0 - tokbase
                        pg, pgB = nextbank()
                        pu, puB = nextbank()
                        for (wv, pt, pB) in ((wg, pg, pgB), (wu, pu, puB)):
                            for kc in range(16):
                                S.add("pe", lambda e, pt=pt, wv=wv, kc=kc, jj=jj, p0=p0, n=n: e.matmul(
                                    pt[:, 0:n], wv[:, kc, jj * 128:(jj + 1) * 128], hT[:, kc, p0:p0 + n],
                                    start=(kc == 0), stop=(kc == 15)), reads=[wb, BhT], writes=[pB])
                        s_ = cnt % 2
                        cnt += 1
                        S.add("act", lambda e, pg=pg, s_=s_, n=n: e.activation(out=sg[s_][:, 0:n], in_=pg[:, 0:n],
                                                                              func=AF.Silu),
                              reads=[pgB], writes=[Bsg[s_]])
                        S.add("dve", lambda e, pu=pu, s_=s_, n=n, j=j, p0=p0: e.tensor_tensor(
                            out=A[:, j, p0:p0 + n], in0=sg[s_][:, 0:n], in1=pu[:, 0:n], op=ALU.mult),
                            reads=[Bsg[s_], puB], writes=[BA])
            gens = [_opr(li, w_ffn_out[li], FC, A, BA, blocks, tokbase, 80, 128)]
            if nxt_blocks is not None:
                S.barrier()
                mark2 = AR.off
                AR.off = 0
                ntb = nxt_blocks[0][0]
                gens.append(norm_gen(li, 1, nxt_blocks, None, None, ntb, nbuf=2, bmax=128, dram_dest=hT_d))
                assert AR.off <= mark, ("norm temps overflow hT region", AR.off, mark)
                AR.off = mark2
            interleave(*gens)
            S.barrier()

        def hgrn2_phase():
            li = 1
            AR.reset()
            hT = AR.bf16(16 * T).rearrange("p (c t) -> p c t", c=16)
            BhT = Buf()
            mark = AR.off
            norm_phase(1, 0, BLK_ALL, hT, BhT, 0, colmajor=True, nbuf=2, bmax=256)
            S.barrier()
            AR.off = mark
            permute_colmajor(hT, BhT, 0)
            S.add("sp", lambda e: e.dma_start(out=hT1_d.rearrange("c p t -> p c t"), in_=hT), reads=[BhT], dma=True)
            S.barrier()
            AR.reset()
            NCH = T // 128
            X1 = [AR.f32(T), AR.f32(T)]; X2 = [AR.f32(T), AR.f32(T)]; X3 = [AR.f32(T), AR.f32(T)]
            q = AR.f32(T)
            smask = AR.bf16(T + 4)
            v16 = AR.bf16(T)
            qe = [AR.bf16(T), AR.bf16(T)]; ke = [AR.bf16(T), AR.bf16(T)]
            vT = AR.bf16(NCH * 128).rearrange("p (c e) -> p c e", c=NCH)
            keT = [AR.bf16(NCH * 128).rearrange("p (c e) -> p c e", c=NCH) for _ in range(2)]
            SqA = [AR.bf16(NCH * 128).rearrange("p (c e) -> p c e", c=NCH) for _ in range(2)]
            att = [AR.bf16(128) for _ in range(4)]
            Rst = [[AR.f32(128), AR.f32(128)], [AR.f32(128), AR.f32(128)]]
            emb = AR.f32(2 * 2 * NCH).rearrange("p (d k c) -> p d k c", d=2, k=2)
            hTs = [AR.bf16(16 * 512).rearrange("p (c t) -> p c t", c=16) for _ in range(2)]
            BX1 = [Buf(), Buf()]; BX2 = [Buf(), Buf()]; BX3 = [Buf(), Buf()]
            Bq, Bmask, Bv16, BvT, Bemb = [Buf() for _ in range(5)]
            Bqe = [Buf(), Buf()]; Bke = [Buf(), Buf()]; BSq = [Buf(), Buf()]; BkeT = [Buf(), Buf()]
            Batt = [Buf() for _ in range(4)]
            BR = [[Buf(), Buf()], [Buf(), Buf()]]
            BhTs = [Buf(), Buf()]
            o = X3[1][:, 0:TL]; Bo = BX3[1]
            rsb = X3[0]; Brsb = BX3[0]
            g = X2[0]; Bg = BX2[0]
            ub = qe[1]; Bub = Bqe[1]
            ubp = ke[0]; Bubp = Bke[0]
            S.add("dve", lambda e: e.memset(smask, 1.0), writes=[Bmask])
            S.add("dve", lambda e: e.memset(fap(smask, 0, [[128, NCH + 1]]), 0.0), writes=[Bmask])
            ORD = [list(range(NCH)), [1, 0] + list(range(NCH - 1, 1, -1))]
            OFFS = [(63, 127), (64, 0)]
            SCALE = 128.0 ** -0.5
            c3 = lambda a: a.rearrange("p (c t) -> p c t", t=128)
            rot = [0]
            hcnt = [0]

            def lin_pass(p):
                doA = p < 16
                doG = p >= 1
                if doA:
                    kA, wbA = ws.next([(j * 2048, (16, 128), wrows(hg_w_in, col0 + p * 128, 128))
                                       for j, col0 in enumerate((2048, 4096, 0, 6144))], ahead=1)
                    wA = wslot[:, kA, 0:8192].rearrange("p (j a b) -> p j a b", j=4, a=16)
                if doG:
                    kG, wbG = ws.next([(0, (16, 128), wrows(hg_w_in, 8192 + (p - 1) * 128, 128))], ahead=1)
                    wG = wslot[:, kG, 0:2048].rearrange("p (a b) -> p a b", a=16)
                for bi, (t0, n, isctx) in enumerate(BLK_ALL):
                    hk = hcnt[0] % 2
                    hcnt[0] += 1
                    S.add("sp", lambda e, hk=hk, t0=t0, n=n: e.dma_start(
                        out=hTs[hk][:, :, 0:n], in_=hT1_d[:, :, t0:t0 + n].rearrange("c p t -> p c t")),
                        writes=[BhTs[hk]], dma=True)
                    yield
                    if doG and not isctx:
                        pt, pB = nextbank()
                        for kc in range(16):
                            S.add("pe", lambda e, pt=pt, kc=kc, hk=hk, n=n: e.matmul(
                                pt[:, 0:n], wG[:, kc, :], hTs[hk][:, kc, 0:n], start=(kc == 0), stop=(kc == 15)),
                                reads=[wbG, BhTs[hk]], writes=[pB])
                        S.add("act", lambda e, pt=pt, t0=t0, n=n: e.activation(out=g[:, t0:t0 + n], in_=pt[:, 0:n], func=AF.Copy),
                              reads=[pB], writes=[Bg])
                        yield
                    if doA:
                        for j in range(4):
                            pt, pB = nextbank()
                            for kc in range(16):
                                S.add("pe", lambda e, pt=pt, kc=kc, hk=hk, n=n, j=j: e.matmul(
                                    pt[:, 0:n], wA[:, j, kc, :], hTs[hk][:, kc, 0:n], start=(kc == 0), stop=(kc == 15)),
                                    reads=[wbA, BhTs[hk]], writes=[pB])
                            if j < 2:
                                S.add("act", lambda e, pt=pt, t0=t0, n=n, j=j: e.activation(
                                    out=X1[j][:, t0:t0 + n], in_=pt[:, 0:n], func=AF.Sigmoid), reads=[pB], writes=[BX1[j]])
                            elif j == 2:
                                S.add("dve", lambda e, pt=pt, t0=t0, n=n: e.tensor_copy(out=q[:, t0:t0 + n], in_=pt[:, 0:n]),
                                      reads=[pB], writes=[Bq])
                            else:
                                S.add("dve", lambda e, pt=pt, t0=t0, n=n: e.tensor_copy(out=v16[:, t0:t0 + n], in_=pt[:, 0:n]),
                                      reads=[pB], writes=[Bv16])
                            yield

            def vtrans():
                for c0 in range(0, NCH, 4):
                    pt, pB = nextbank()
                    ptb = pt[:, 0:256].bitcast(BF16).rearrange("p (a b) -> p a b", a=4)
                    nn = min(4, NCH - c0)
                    for j in range(nn):
                        S.add("pe", lambda e, ptb=ptb, j=j, c=c0 + j: e.transpose(
                            out=ptb[:, j, :], in_=v16[:, c * 128:(c + 1) * 128], identity=ident_b),
                            reads=[Bv16, Bcbf], writes=[pB])
                    S.add("act", lambda e, ptb=ptb, c0=c0, nn=nn: e.activation(
                        out=vT[:, c0:c0 + nn, :], in_=ptb[:, 0:nn, :], func=AF.Copy), reads=[pB], writes=[BvT])
                    yield

            def chain(n_, d):
                x1, x2, x3 = X1[d], X2[d], X3[d]
                B1, B2, B3 = BX1[d], BX2[d], BX3[d]
                lbc = lambda k_: lbv[:, k_, d * 16 + n_:d * 16 + n_ + 1]
                moff, loff = OFFS[d]
                S.add("act", lambda e: e.activation(out=x2, in_=x1, func=AF.Identity, scale=lbc(2), bias=lbc(1)),
                      reads=[B1, Blb], writes=[B2])
                yield
                S.add("act", lambda e: e.activation(out=x3, in_=x1, func=AF.Ln, scale=lbc(1), bias=lbc(0)),
                      reads=[B1, Blb], writes=[B3])
                yield
                if d == 0:
                    S.add("dve", lambda e: e.tensor_tensor_scan(out=x1, data0=smask[:, 0:T], data1=x3, initial=0.0,
                                                                op0=ALU.mult, op1=ALU.add),
                          reads=[B3, Bmask], writes=[B1])
                else:
                    rv = lambda a, o_=0: fap(a, T - 1 + o_, [[-1, T]])
                    S.add("dve", lambda e: e.tensor_tensor_scan(out=rv(x1), data0=rv(smask, 1), data1=rv(x3),
                                                                initial=0.0, op0=ALU.mult, op1=ALU.add),
                          reads=[B3, Bmask], writes=[B1])
                yield
                sv = lambda off, c0=0, cn=NCH: fap(x1, off + c0 * 128, [[128, cn]])
                dd = emb[:, d, 1, :]
                G = emb[:, d, 0, :]
                S.add("dve", lambda e: e.tensor_tensor(out=dd, in0=sv(loff), in1=sv(moff), op=ALU.subtract),
                      reads=[B1], writes=[Bemb])
                if d == 0:
                    S.add("dve", lambda e: e.tensor_tensor(out=G[:, 0:NCH - 1], in0=sv(moff, 1, NCH - 1), in1=dd[:, 0:NCH - 1],
                                                           op=ALU.add), reads=[B1, Bemb], writes=[Bemb])
                else:
                    S.add("dve", lambda e: e.tensor_tensor(out=G[:, 3:NCH], in0=sv(moff, 2, NCH - 3), in1=dd[:, 3:NCH],
                                                           op=ALU.add), reads=[B1, Bemb], writes=[Bemb])
                    S.add("dve", lambda e: e.tensor_tensor(out=G[:, 1:2], in0=sv(moff, 0, 1), in1=dd[:, 1:2], op=ALU.add),
                          reads=[B1, Bemb], writes=[Bemb])
                    S.add("dve", lambda e: e.tensor_tensor(out=G[:, 0:1], in0=sv(moff, NCH - 1, 1), in1=dd[:, 0:1], op=ALU.add),
                          reads=[B1, Bemb], writes=[Bemb])
                    S.add("dve", lambda e: e.memset(G[:, 2:3], 0.0), writes=[Bemb])
                if d == 0:
                    S.add("dve", lambda e: e.memset(G[:, NCH - 1:NCH], 0.0), writes=[Bemb])
                S.add("act", lambda e: e.activation(out=G, in_=G, func=AF.Exp), reads=[Bemb], writes=[Bemb])
                S.add("dve", lambda e: e.tensor_tensor(out=c3(x3), in0=c3(x1), in1=fap(x1, moff, [[128, NCH], [0, 128]]),
                                                       op=ALU.subtract), reads=[B1], writes=[B3])
                yield
                S.add("act", lambda e: e.activation(out=x1, in_=x3, func=AF.Exp), reads=[B3, Bemb], writes=[B1])
                yield
                S.add("act", lambda e: e.activation(out=x3, in_=x3, func=AF.Exp, scale=-1.0), reads=[B3], writes=[B3])
                yield
                S.add("dve", lambda e: e.scalar_tensor_tensor(out=qe[d], in0=q, scalar=SCALE, in1=x1,
                                                               op0=ALU.mult, op1=ALU.mult),
                      reads=[Bq, B1], writes=[Bqe[d]])
                yield
                S.add("dve", lambda e: e.tensor_tensor(out=ke[d], in0=x2, in1=x3, op=ALU.mult),
                      reads=[B2, B3], writes=[Bke[d]])
                yield
                for c0 in range(0, NCH, 4):
                    pt, pB = nextbank()
                    ptb = pt[:, 0:256].bitcast(BF16).rearrange("p (a b) -> p a b", a=4)
                    nn = min(4, NCH - c0)
                    for j in range(nn):
                        S.add("pe", lambda e, ptb=ptb, j=j, c=c0 + j: e.transpose(
                            out=ptb[:, j, :], in_=ke[d][:, c * 128:(c + 1) * 128], identity=ident_b),
                            reads=[Bke[d], Bcbf], writes=[pB])
                    S.add("act", lambda e, ptb=ptb, c0=c0, nn=nn: e.activation(
                        out=keT[d][:, c0:c0 + nn, :], in_=ptb[:, 0:nn, :], func=AF.Copy), reads=[pB], writes=[BkeT[d]])
                    yield

            def chunk_stage(n_):
                for s0 in range(0, NCH - 1, 4):
                    steps = list(range(s0, min(s0 + 4, NCH - 1)))
                    pds = []
                    for d in range(2):
                        pd, pdB = nextbank()
                        pds.append((pd, pdB))
                        for j, step in enumerate(steps):
                            c = ORD[d][step]
                            S.add("pe", lambda e, pd=pd, j=j, c=c, d=d: e.matmul(
                                pd[:, j * 128:(j + 1) * 128], keT[d][:, c, :], vT[:, c, :], start=True, stop=True),
                                reads=[BkeT[d], BvT], writes=[pdB])
                    yield
                    for j, step in enumerate(steps):
                        for d in range(2):
                            c = ORD[d][step]
                            cn = ORD[d][step + 1]
                            pd, pdB = pds[d]
                            cur, nxt = Rst[d][step % 2], Rst[d][(step + 1) % 2]
                            Bcur, Bnxt = BR[d][step % 2], BR[d][(step + 1) % 2]
                            if step == 0:
                                S.add("dve", lambda e, pd=pd, j=j, nxt=nxt: e.tensor_copy(out=nxt, in_=pd[:, j * 128:(j + 1) * 128]),
                                      reads=[pdB], writes=[Bnxt])
                            else:
                                cp = ORD[d][step - 1]
                                S.add("dve", lambda e, pd=pd, j=j, d=d, cp=cp, cur=cur, nxt=nxt: e.scalar_tensor_tensor(
                                    out=nxt, in0=cur, scalar=emb[:, d, 0, cp:cp + 1], in1=pd[:, j * 128:(j + 1) * 128],
                                    op0=ALU.mult, op1=ALU.add), reads=[Bcur, Bemb, pdB], writes=[Bnxt])
                            if cn >= 2:
                                S.add("pool", lambda e, d=d, c=c, cn=cn, nxt=nxt: e.tensor_scalar(
                                    out=SqA[d][:, cn, :], in0=nxt, scalar1=emb[:, d, 0, c:c + 1], scalar2=0.0,
                                    op0=ALU.mult, op1=ALU.add), reads=[Bnxt, Bemb], writes=[BSq[d]])
                        yield

                def att_stage(c):
                    ams = []
                    for d in range(2):
                        pa, paB = nextbank()
                        S.add("pe", lambda e, pa=pa, d=d: e.matmul(
                            pa[:, 0:128], ke[d][:, c * 128:(c + 1) * 128], qe[d][:, c * 128:(c + 1) * 128],
                            start=True, stop=True), reads=[Bke[d], Bqe[d]], writes=[paB])
                        r = rot[0] % 4
                        rot[0] += 1
                        msk = trif_b if d == 0 else trib_b
                        S.add("dve", lambda e, pa=pa, r=r, msk=msk: e.tensor_tensor(out=att[r], in0=pa[:, 0:128], in1=msk,
                                                                                    op=ALU.mult),
                              reads=[paB, Bcbf], writes=[Batt[r]])
                        ams.append(r)
                    return ams

                def o_stage(c, ams):
                    po, poB = nextbank()
                    for d in range(2):
                        r = ams[d]
                        S.add("pe", lambda e, po=po, r=r, d=d: e.matmul(po[:, 0:128], vT[:, c, :], att[r],
                                                                        start=(d == 0), stop=False),
                              reads=[BvT, Batt[r]], writes=[poB])
                        S.add("pe", lambda e, po=po, d=d: e.matmul(
                            po[:, 0:128], SqA[d][:, c, :], qe[d][:, c * 128:(c + 1) * 128], start=False, stop=(d == 1)),
                            reads=[BSq[d], Bqe[d]], writes=[poB])
                    S.add("act", lambda e, po=po: e.activation(out=o[:, (c - 2) * 128:(c - 1) * 128], in_=po[:, 0:128],
                                                               func=AF.Copy), reads=[poB], writes=[Bo])

                prev = att_stage(2)
                for c in range(2, NCH):
                    nxt_ = att_stage(c + 1) if c + 1 < NCH else None
                    o_stage(c, prev)
                    prev = nxt_
                    yield

            def tail(n_):
                sqb = qe[0]
                S.add("act", lambda e: e.activation(out=sqb[:, 0:TL], in_=o, func=AF.Square), reads=[Bo], writes=[Bqe[0]])
                for bi in range(4):
                    pt, pB = nextbank()
                    S.add("pe", lambda e, pt=pt, bi=bi: e.matmul(pt[:, 0:512], ones_b, sqb[:, bi * 512:(bi + 1) * 512],
                                                                 start=True, stop=True), reads=[Bqe[0], Bcbf], writes=[pB])
                    S.add("act", lambda e, pt=pt, bi=bi: e.activation(out=rsb[:, bi * 512:(bi + 1) * 512], in_=pt[:, 0:512],
                                                                      func=AF.Ln, scale=1.0 / 128, bias=epsc),
                          reads=[pB, Bsmall2], writes=[Brsb])
                S.add("act", lambda e: e.activation(out=rsb[:, 0:TL], in_=rsb[:, 0:TL], func=AF.Exp, scale=-0.5),
                      reads=[Brsb], writes=[Brsb])
                S.add("dve", lambda e: e.tensor_tensor(out=o, in0=o, in1=rsb[:, 0:TL], op=ALU.mult), reads=[Bo, Brsb], writes=[Bo])
                S.add("act", lambda e: e.activation(out=g[:, TC:T], in_=g[:, TC:T], func=AF.Silu), reads=[Bg], writes=[Bg])
                S.add("dve", lambda e: e.scalar_tensor_tensor(
                    out=ub[:, TC:T], in0=g[:, TC:T], scalar=pvs("hgnorm"), in1=o, op0=ALU.mult, op1=ALU.mult),
                    reads=[Bg, Bo, Bpv], writes=[Bub])
                S.add("pool", lambda e: e.tensor_copy(out=ubp[:, TC:T].rearrange("p (r w) -> p r w", w=64),
                                                      in_=fap(ub, TC, [[1, 32], [32, 64]])),
                      reads=[Bub], writes=[Bubp])
                S.add("sp", lambda e: e.dma_start(out=uT_d[n_][:, TC:T], in_=ubp[:, TC:T]), reads=[Bubp], dma=True)

            interleave(lin_pass(0))
            for n_ in range(16):
                c0_ = chain(n_, 0)
                for _ in range(2):
                    next(c0_)
                interleave(c0_, chain(n_, 1), vtrans())
                interleave(chunk_stage(n_), lin_pass(n_ + 1))
                tail(n_)
            S.barrier()
            dbg_dump("uT1", uT_d)
            if stop_after == "hgB":
                return
            AR.reset()
            U = AR.bf16(16 * TL).rearrange("p (c t) -> p c t", c=16)
            BU = Buf()
            S.add("sp", lambda e: e.dma_start(out=U, in_=uT_d[:, :, TC:T].rearrange("c p t -> p c t")), writes=[BU], dma=True)
            out_proj_residual(1, hg_w_out, 16, U, BU, BLK_LAT, TC, 32, 512)
            S.barrier()

        def final_phase():
            AR.reset()
            xb_ = [AR.f32(16 * 512).rearrange("p (c t) -> p c t", c=16) for _ in range(2)]
            Bx = [Buf(), Buf()]
            sq = [AR.bf16(16 * 512).rearrange("p (c t) -> p c t", c=16) for _ in range(2)]
            Bsq = [Buf(), Buf()]
            rs = [AR.f32(512), AR.f32(512)]
            Brs = [Buf(), Buf()]
            ot = [AR.f32(2048) for _ in range(2)]
            Bot = [Buf(), Buf()]
            oi = [0]

            def s1(bi):
                t0, n, _ = BLK_LAT[bi]
                k = bi % 2
                S.add("sp", lambda e: e.dma_start(
                    out=xb_[k][:, :, 0:n], in_=xT_d[:, :, t0:t0 + n].rearrange("c p t -> p c t")), writes=[Bx[k]], dma=True)
                S.add("act", lambda e: e.activation(out=sq[k], in_=xb_[k], func=AF.Square), reads=[Bx[k]], writes=[Bsq[k]])
                pt, pB = nextbank()
                for c in range(16):
                    S.add("pe", lambda e, pt=pt, c=c: e.matmul(pt[:, 0:512], ones_b, sq[k][:, c, :], start=(c == 0), stop=(c == 15)),
                          reads=[Bsq[k], Bcbf], writes=[pB])
                S.add("act", lambda e, pt=pt: e.activation(out=rs[k], in_=pt[:, 0:512], func=AF.Sqrt, scale=1.0 / D, bias=epsc),
                      reads=[pB, Bsmall2], writes=[Brs[k]])
                S.add("dve", lambda e: e.reciprocal(out=rs[k], in_=rs[k]), reads=[Brs[k]], writes=[Brs[k]])

            def s2(bi):
                k = bi % 2
                for c in range(16):
                    S.add("dve", lambda e, c=c: e.scalar_tensor_tensor(
                        out=xb_[k][:, c, :], in0=xb_[k][:, c, :], scalar=pvs("gfin", c, 1), in1=rs[k],
                        op0=ALU.mult, op1=ALU.mult), reads=[Bx[k], Brs[k], Bpv], writes=[Bx[k]])
                    if c % 4 == 3:
                        yield

            def s3(bi):
                t0, n, _ = BLK_LAT[bi]
                k = bi % 2
                for sub in range(4):
                    ok = oi[0] % 2
                    oi[0] += 1
                    for qd in range(4):
                        pt, pB = nextbank()
                        for j in range(4):
                            c = qd * 4 + j
                            S.add("pe", lambda e, pt=pt, j=j, c=c, sub=sub: e.transpose(
                                out=pt[:, j * 128:(j + 1) * 128], in_=xb_[k][:, c, sub * 128:(sub + 1) * 128], identity=ident_f),
                                reads=[Bx[k], Bcst], writes=[pB])
                        S.add("act", lambda e, pt=pt, ok=ok, qd=qd: e.activation(out=ot[ok][:, qd * 512:(qd + 1) * 512],
                                                                                 in_=pt[:, :], func=AF.Copy),
                              reads=[pB], writes=[Bot[ok]])
                    tok0 = t0 - TC + sub * 128
                    finals.append(S.add("sp", lambda e, ok=ok, tok0=tok0: e.dma_start(out=out_d[tok0:tok0 + 128, :], in_=ot[ok]),
                                        reads=[Bot[ok]], dma=True))
                    yield

            nb = len(BLK_LAT)
            s1(0)
            interleave(s2(0))
            s1(1)
            for bi in range(nb):
                gens = [s3(bi)]
                if bi + 1 < nb:
                    gens.append(s2(bi + 1))
                interleave(*gens)
                if bi + 2 < nb:
                    s1(bi + 2)


        AR.reset()
        hT = AR.bf16(16 * T).rearrange("p (c t) -> p c t", c=16)
        BhT = Buf()
        mark = AR.off
        norm_phase(0, 0, BLK_ALL, hT, BhT, 0, nbuf=2, bmax=256)
        S.barrier()
        AR.off = mark
        linear_to_dram(hT, BhT, BLK_ALL, 0, rg_w_in, 32, lambda oc: F_d[oc], lambda oc: F32)
        S.barrier()
        dbg_dump("F0", F_d)
        if stop_after == "rgA":
            return
        AR.reset()
        xb = [AR.f32(T), AR.f32(T)]; gg = [AR.f32(T), AR.f32(T), AR.f32(T)]
        xc = [AR.f32(T), AR.f32(T)]; gt = AR.f32(T)
        ra = [AR.f32(T), AR.f32(T)]; ig = [AR.f32(T), AR.f32(T)]; hd = [AR.f32(T), AR.f32(T)]
        xcb = [AR.bf16(T), AR.bf16(T)]; ub = AR.bf16(T)
        Bxb = [Buf(), Buf()]; Bgg = [Buf(), Buf(), Buf()]; Bxc = [Buf(), Buf()]; Bxcb = [Buf(), Buf()]
        Bgt, Bub = Buf(), Buf()
        Bra = [Buf(), Buf()]; Big = [Buf(), Buf()]; Bhd = [Buf(), Buf()]
        SEG = [(0, TC), (TC, T)]

        def rg_A(n_):
            par = n_ % 2
            xb_, gg_, xc_, xcb_ = xb[par], gg[n_ % 3], xc[par], xcb[par]
            Bxb_, Bgg_, Bxc_, Bxcb_ = Bxb[par], Bgg[n_ % 3], Bxc[par], Bxcb[par]
            cw = lambda kk: pvs("convw", kk * 16 + n_, 1)
            S.add("pool", lambda e: e.tensor_scalar(out=xc_, in0=xb_, scalar1=cw(2), scalar2=pvs("convb", n_, 1),
                                                    op0=ALU.mult, op1=ALU.add),
                  reads=[Bxb_, Bpv], writes=[Bxc_])
            yield
            for (s0, s1) in SEG:
                for (kk, sh) in ((1, 1), (0, 2), (3, -1)):
                    if sh > 0:
                        o_ = xc_[:, s0 + sh:s1]; i_ = xb_[:, s0:s1 - sh]
                    else:
                        o_ = xc_[:, s0:s1 + sh]; i_ = xb_[:, s0 - sh:s1]
                    S.add("dve", lambda e, o_=o_, i_=i_, kk=kk: e.scalar_tensor_tensor(
                        out=o_, in0=i_, scalar=cw(kk), in1=o_, op0=ALU.mult, op1=ALU.add),
                        reads=[Bxb_, Bxc_, Bpv], writes=[Bxc_])
                    yield
            S.add("pool", lambda e: e.tensor_copy(out=xcb_, in_=xc_), reads=[Bxc_], writes=[Bxcb_])
            yield
            S.add("act", lambda e: e.activation(out=gt, in_=gg_, func=AF.Square, scale=0.044715 ** 0.5),
                  reads=[Bgg_], writes=[Bgt])
            yield
            S.add("dve", lambda e: e.scalar_tensor_tensor(out=gt, in0=gt, scalar=1.0, in1=gg_, op0=ALU.add, op1=ALU.mult),
                  reads=[Bgt, Bgg_], writes=[Bgt])
            yield
            S.add("act", lambda e: e.activation(out=gt, in_=gt, func=AF.Sigmoid, scale=1.5957691216057308),
                  reads=[Bgt], writes=[Bgt])
            yield
            S.add("dve", lambda e: e.tensor_tensor(out=gg_, in0=gg_, in1=gt, op=ALU.mult), reads=[Bgt, Bgg_], writes=[Bgg_])
            yield

        def rg_BC(n_):
            par = n_ % 2
            xc_, xcb_, Bxc_, Bxcb_ = xc[par], xcb[par], Bxc[par], Bxcb[par]
            k, wb = ws.next([(0, (2, 128), rg_w_a[:, n_].rearrange("d p e -> p d e")),
                             (256, (2, 128), rg_w_i[:, n_].rearrange("d p e -> p d e"))])
            wga = wslot[:, k, 0:256].rearrange("p (a b) -> p a b", a=2)
            wgi = wslot[:, k, 256:512].rearrange("p (a b) -> p a b", a=2)
            for d in range(2):
                h = hd[d]; Bh = Bhd[d]; ra_ = ra[d]; ig_ = ig[d]; Bra_ = Bra[d]; Big_ = Big[d]
                for (wgt, dst, Bdst, bname) in ((wga, ra_, Bra_, "ba"), (wgi, ig_, Big_, "bi")):
                    for (t0, n, isctx) in BLK_ALL:
                        pt, pB = nextbank()
                        S.add("pe", lambda e, pt=pt, wgt=wgt, t0=t0, n=n, d=d: e.matmul(
                            pt[:, 0:n], wgt[:, d, :], xcb_[:, t0:t0 + n], start=True, stop=True),
                            reads=[wb, Bxcb_], writes=[pB])
                        S.add("act", lambda e, pt=pt, dst=dst, t0=t0, n=n, bname=bname, d=d: e.activation(
                            out=dst[:, t0:t0 + n], in_=pt[:, 0:n], func=AF.Sigmoid, bias=pvs(bname, d * 16 + n_, 1)),
                            reads=[pB, Bpv], writes=[Bdst])
                        yield
                S.add("act", lambda e, ra_=ra_, d=d: e.activation(out=ra_, in_=ra_, func=AF.Exp,
                                                                   scale=c1[:, d * 16 + n_:d * 16 + n_ + 1]),
                      reads=[Bra_, Bc1], writes=[Bra_])
                yield
                S.add("act", lambda e, h=h, ra_=ra_: e.activation(out=h, in_=ra_, func=AF.Square), reads=[Bra_], writes=[Bh])
                yield
                S.add("act", lambda e, h=h: e.activation(out=h, in_=h, func=AF.Sqrt, scale=-1.0, bias=onec),
                      reads=[Bh, Bsmall2], writes=[Bh])
                yield
                S.add("dve", lambda e, h=h, ig_=ig_: e.tensor_tensor(out=ig_, in0=ig_, in1=h, op=ALU.mult),
                      reads=[Big_, Bh], writes=[Big_])
                yield
                S.add("dve", lambda e, ig_=ig_: e.tensor_tensor(out=ig_, in0=ig_, in1=xc_, op=ALU.mult),
                      reads=[Big_, Bxc_], writes=[Big_])
                yield
                if d == 0:
                    S.add("dve", lambda e, h=h, ra_=ra_, ig_=ig_: e.tensor_tensor_scan(
                        out=h, data0=ra_, data1=ig_, initial=0.0, op0=ALU.mult, op1=ALU.add),
                        reads=[Bra_, Big_], writes=[Bh])
                else:
                    rv = lambda a, s0, s1: fap(a, s1 - 1, [[-1, s1 - s0]])
                    S.add("dve", lambda e, h=h, ra_=ra_, ig_=ig_, rv=rv: e.tensor_tensor_scan(
                        out=rv(h, 0, TC), data0=rv(ra_, 0, TC), data1=rv(ig_, 0, TC), initial=0.0,
                        op0=ALU.mult, op1=ALU.add), reads=[Bra_, Big_], writes=[Bh])
                    S.add("dve", lambda e, h=h, ra_=ra_, ig_=ig_, rv=rv: e.tensor_tensor_scan(
                        out=rv(h, TC, T), data0=rv(ra_, TC, T), data1=rv(ig_, TC, T), initial=h[:, 0:1],
                        op0=ALU.mult, op1=ALU.add), reads=[Bra_, Big_, Bh], writes=[Bh])
                yield

        def rg_loads(n_):
            S.add("sp", lambda e: e.dma_start(out=xb[n_ % 2], in_=F_d[n_]), writes=[Bxb[n_ % 2]], dma=True)
            S.add("sp", lambda e: e.dma_start(out=gg[n_ % 3], in_=F_d[16 + n_]), writes=[Bgg[n_ % 3]], dma=True)

        rg_loads(0)
        interleave(rg_A(0))
        rg_loads(1)
        for n_ in range(16):
            if n_ + 2 < 16:
                rg_loads(n_ + 2)
            gens = [rg_BC(n_)]
            if n_ + 1 < 16:
                gens.append(rg_A(n_ + 1))
            interleave(*gens)
            gg_ = gg[n_ % 3]
            S.add("dve", lambda e: e.tensor_tensor(out=hd[0], in0=hd[0], in1=hd[1], op=ALU.add),
                  reads=[Bhd[0], Bhd[1]], writes=[Bhd[0]])
            S.add("dve", lambda e, gg_=gg_: e.tensor_tensor(out=ub, in0=hd[0], in1=gg_, op=ALU.mult),
                  reads=[Bhd[0], Bgg[n_ % 3]], writes=[Bub])
            S.add("sp", lambda e, n_=n_: e.dma_start(out=uT_d[n_], in_=ub), reads=[Bub], dma=True)
            for _ in range(3):
                next(ada1, None)
        interleave(ada1)
        S.barrier()
        dbg_dump("uT0", uT_d)
        if stop_after == "rgB":
            return
        AR.reset()
        U = AR.bf16(16 * T).rearrange("p (c t) -> p c t", c=16)
        BU = Buf()
        S.add("sp", lambda e: e.dma_start(out=U, in_=uT_d.rearrange("c p t -> p c t")), writes=[BU], dma=True)
        out_proj_residual(0, rg_w_out, 16, U, BU, BLK_ALL, 0, 32, 512)
        S.barrier()
        dbg_dump("xT1", xT_d)
        if stop_after == "mix0":
            return
        ffn_phase(0, [[(0, 256, 1), (256, 512, 0), (768, 384, 0)], [(1152, 512, 0), (1664, 512, 0), (2176, 128, 0)]])
        dbg_dump("xT2", xT_d)
        if stop_after == "ffn0":
            return
        hgrn2_phase()
        dbg_dump("xT3", xT_d)
        if stop_after == "mix1":
            return
        ffn_phase(1, [[(256, 512, 0), (768, 512, 0)], [(1280, 512, 0), (1792, 512, 0)]])
        dbg_dump("xT4", xT_d)
        final_phase()

    S.plan = True
    try:
        program()
    except NotImplementedError:
        pass
    S.plan = False
    pbi[0] = 0
    try:
        program()
    except NotImplementedError:
        pass
    S.emit(final_waits=finals)
    return nc, list(dbg_out.keys())


def host_inputs(inputs, b):
    pvv = np.zeros((128, NPV), np.float32)

    def put(name, arr):
        a = _fm(arr)
        pvv[:, PV_OFF[name]:PV_OFF[name] + a.shape[1]] = a

    put("c", inputs["c"][b])
    put("cctx", inputs["c_ctx"])
    put("bada", inputs["b_ada"])
    put("gmix", inputs["g_mix"])
    put("gffn", inputs["g_ffn"])
    put("gfin", inputs["g_final"])
    put("convw", inputs["rg_conv_w"][0])
    put("convb", inputs["rg_conv_b"][0])
    put("ba", np.asarray(inputs["rg_b_a"][0]).reshape(2, 2048))
    put("bi", np.asarray(inputs["rg_b_i"][0]).reshape(2, 2048))
    put("lam", inputs["rg_lam"][0])
    put("hglb", inputs["hg_lb"])
    put("hgnorm", inputs["hg_norm"][0])
    f = lambda a: np.ascontiguousarray(np.asarray(a, np.float32))
    return {
        "x": f(inputs["x"][b]), "ctx": f(inputs["ctx"][b]), "pv": pvv, "cst": _consts(),
        "w_ada": f(inputs["w_ada"]), "w_ffn_in": f(inputs["w_ffn_in"]), "w_ffn_out": f(inputs["w_ffn_out"]),
        "rg_w_in": f(inputs["rg_w_in"][0]), "rg_w_a": f(inputs["rg_w_a"][0]), "rg_w_i": f(inputs["rg_w_i"][0]),
        "rg_w_out": f(inputs["rg_w_out"][0]), "hg_w_in": f(inputs["hg_w_in"][0]), "hg_w_out": f(inputs["hg_w_out"][0]),
    }


def kernel(**inputs):
    nc, _ = build()
    in_maps = [host_inputs(inputs, b) for b in range(8)]
    res = run_bass_kernel_spmd(nc, in_maps, core_ids=list(range(8)))
    return np.stack([np.asarray(r["out"], np.float32) for r in res.results], axis=0)
```

```python
import contextlib
import itertools
import numpy as np
import concourse.bass as bass
import concourse.mybir as mybir
from concourse.bass_utils import run_bass_kernel_spmd

F32 = mybir.dt.float32
BF16 = mybir.dt.bfloat16
AF = mybir.ActivationFunctionType
ALU = mybir.AluOpType
AX = mybir.AxisListType

D = 2048
KC = 16
TC = 256
TL = 2048
T = TC + TL
FF = 5632
FC = 44
EPS = 1e-6
STREAMS = ["pe", "act", "dve", "pool", "sp"]


class Buf:
    __slots__ = ("name", "w", "rc", "rd")

    def __init__(self, name=""):
        self.name = name
        self.w = None
        self.rc = {}
        self.rd = []


class Op:
    __slots__ = ("st", "fn", "deps", "dma", "sig", "sem", "val", "pre")

    def __init__(self, st, fn, dma):
        self.st = st
        self.fn = fn
        self.dma = dma
        self.deps = []
        self.sig = False
        self.sem = None
        self.val = 0
        self.pre = None


class Sched:
    NDMASEM = 8

    def __init__(self, nc):
        self.nc = nc
        self.ops = {s: [] for s in STREAMS}
        self.pending = {s: [] for s in STREAMS}
        self.dma_since = {s: [] for s in STREAMS}
        self.plan = False

    def add(self, st, fn, reads=(), writes=(), dma=False):
        if self.plan:
            return None
        op = Op(st, fn, dma)
        deps = []

        def dep(d, hard):
            if d is None or d is op:
                return
            if (not hard) and (not d.dma) and (not dma) and d.st == st:
                return
            deps.append(d)

        for b in reads:
            dep(b.w, True)
        for b in writes:
            dep(b.w, False)
            for r in b.rc.values():
                dep(r, False)
            for r in b.rd:
                dep(r, False)
        for b in reads:
            if dma:
                b.rd.append(op)
            else:
                b.rc[st] = op
        for b in writes:
            b.w = op
            b.rc = {}
            b.rd = []
        if self.pending[st]:
            deps.extend(self.pending[st])
            self.pending[st] = []
        op.deps = deps
        for d in deps:
            d.sig = True
        if dma:
            op.sig = True
            self.dma_since[st].append(op)
        self.ops[st].append(op)
        return op

    def barrier(self, streams=("pe", "act", "dve", "sp")):
        if self.plan:
            return
        fr = {}
        for s in streams:
            f = list(self.dma_since[s])
            self.dma_since[s] = []
            for op in reversed(self.ops[s]):
                if not op.dma:
                    f.append(op)
                    break
            fr[s] = f
        for s in streams:
            for s2 in streams:
                if s2 != s:
                    self.pending[s].extend(fr[s2])
                else:
                    self.pending[s].extend([o for o in fr[s2] if o.dma])

    def emit(self, final_waits=()):
        nc = self.nc
        es = contextlib.ExitStack()
        csem = {s: es.enter_context(nc.semaphore("c_" + s)) for s in STREAMS}
        dsem = {s: [es.enter_context(nc.semaphore("d_%s%d" % (s, i))) for i in range(self.NDMASEM)]
                for s in STREAMS}
        for s in STREAMS:
            cnt = 0
            dcount = [0] * self.NDMASEM
            di = 0
            for op in self.ops[s]:
                if op.dma:
                    k = di % self.NDMASEM
                    di += 1
                    if dcount[k] > 0:
                        op.pre = (dsem[s][k], dcount[k] * 16)
                    dcount[k] += 1
                    op.sem = dsem[s][k]
                    op.val = dcount[k] * 16
                elif op.sig:
                    cnt += 1
                    op.sem = csem[s]
                    op.val = cnt
        final = {}
        for op in final_waits:
            if op is not None:
                final.setdefault(op.st, []).append(op)
        block = es.enter_context(nc.Block())

        def run_stream(s):
            def body(e):
                known = {}

                def wait(sem, val):
                    key = id(sem)
                    if known.get(key, 0) >= val:
                        return
                    known[key] = val
                    e.wait_ge(sem, val)

                for op in self.ops[s]:
                    if op.pre is not None:
                        wait(*op.pre)
                    for d in op.deps:
                        wait(d.sem, d.val)
                    ins = op.fn(e)
                    if op.sig:
                        ins.then_inc(op.sem, 16 if op.dma else 1)
                for op in final.get(s, []):
                    wait(op.sem, op.val)
            return body

        block.tensor(run_stream("pe"))
        block.scalar(run_stream("act"))
        block.vector(run_stream("dve"))
        block.gpsimd(run_stream("pool"))
        block.sync(run_stream("sp"))
        es.close()


PV_SEGS = [("c", 16), ("cctx", 16), ("bada", 192), ("gmix", 32), ("gffn", 32), ("gfin", 16),
           ("convw", 64), ("convb", 16), ("ba", 32), ("bi", 32), ("lam", 32), ("hglb", 64), ("hgnorm", 1)]
PV_OFF = {}
_o = 0
for _n, _s in PV_SEGS:
    PV_OFF[_n] = _o
    _o += _s
NPV = _o
CST_SEGS = [("ident", 128), ("trif", 128), ("trib", 128), ("ones", 128), ("smask", T + 4)]
CST_OFF = {}
_o = 0
for _n, _s in CST_SEGS:
    CST_OFF[_n] = _o
    _o += _s
NCST = _o


def _fm(v):
    v = np.asarray(v, np.float32)
    lead = v.shape[:-1]
    k = v.shape[-1] // 128
    v = v.reshape(lead + (k, 128))
    v = np.moveaxis(v, -1, 0)
    return np.ascontiguousarray(v).reshape(128, -1)


def _consts():
    c = np.zeros((128, NCST), np.float32)
    i = np.arange(128)
    c[:, CST_OFF["ident"]:CST_OFF["ident"] + 128] = np.eye(128, dtype=np.float32)
    c[:, CST_OFF["trif"]:CST_OFF["trif"] + 128] = (i[None, :] >= i[:, None])
    c[:, CST_OFF["trib"]:CST_OFF["trib"] + 128] = (i[None, :] <= i[:, None])
    c[:, CST_OFF["ones"]:CST_OFF["ones"] + 128] = 1.0
    u = np.arange(T + 4)
    c[:, CST_OFF["smask"]:CST_OFF["smask"] + T + 4] = (u % 128 != 0)[None, :]
    return c


def build(debug=(), stop_after=None):
    nc = bass.Bass("TRN2", target_bir_lowering=False)
    S = Sched(nc)

    def din(name, shape, dt=F32):
        return nc.dram_tensor(name, list(shape), dt, kind="ExternalInput").ap()

    x_d = din("x", [TL, D])
    ctx_d = din("ctx", [TC, D])
    pv_d = din("pv", [128, NPV])
    cst_d = din("cst", [128, NCST])
    w_ada = din("w_ada", [2, D, 6 * D])
    w_ffn_in = din("w_ffn_in", [2, D, 2 * FF])
    w_ffn_out = din("w_ffn_out", [2, FF, D])
    rg_w_in = din("rg_w_in", [D, 2 * D])
    rg_w_a = din("rg_w_a", [2, 16, 128, 128])
    rg_w_i = din("rg_w_i", [2, 16, 128, 128])
    rg_w_out = din("rg_w_out", [D, D])
    hg_w_in = din("hg_w_in", [D, 5 * D])
    hg_w_out = din("hg_w_out", [D, D])
    out_d = nc.dram_tensor("out", [TL, D], F32, kind="ExternalOutput").ap()
    xT_d = nc.dram_tensor("xT_s", [KC, 128, T], F32, kind="Internal").ap()
    uT_d = nc.dram_tensor("uT_s", [KC, 128, T], BF16, kind="Internal").ap()
    F_d = nc.dram_tensor("F_s", [80, 128, T], F32, kind="Internal").ap()
    Fv_d = nc.dram_tensor("Fv_s", [16, 128, T], BF16, kind="Internal").ap()
    hT_d = nc.dram_tensor("hT_s", [16, 128, 1152], BF16, kind="Internal").ap()
    hT1_d = nc.dram_tensor("hT1_s", [16, 128, T], BF16, kind="Internal").ap()
    dbg_out = {}
    for name in debug:
        if name.startswith("xT"):
            dbg_out[name] = nc.dram_tensor("dbg_" + name, [KC, 128, T], F32, kind="ExternalOutput").ap()
        elif name.startswith("F"):
            dbg_out[name] = nc.dram_tensor("dbg_" + name, [80, 128, T], F32, kind="ExternalOutput").ap()
        elif name.startswith("uT"):
            dbg_out[name] = nc.dram_tensor("dbg_" + name, [KC, 128, T], BF16, kind="ExternalOutput").ap()
        elif name == "small":
            dbg_out[name] = nc.dram_tensor("dbg_small", [128, 1024], F32, kind="ExternalOutput").ap()
    finals = []

    pv = nc.alloc_sbuf_tensor("pvsb", [128, NPV], F32)
    cst = nc.alloc_sbuf_tensor("cstsb", [128, NCST - (T + 4)], F32)
    modv = nc.alloc_sbuf_tensor("modv", [128, 2, 96, 2], F32)
    gm = nc.alloc_sbuf_tensor("gm", [128, 2, 2, 16, 2], F32)
    gfb = nc.alloc_sbuf_tensor("gfb", [128, 16], F32)
    c1 = nc.alloc_sbuf_tensor("c1", [128, 32], F32)
    lbv = nc.alloc_sbuf_tensor("lbv", [128, 3, 32], F32)
    scb = nc.alloc_sbuf_tensor("scb", [128, 16, 2], BF16)
    cbf = nc.alloc_sbuf_tensor("cbf", [128, 4, 128], BF16)
    small = nc.alloc_sbuf_tensor("smallt", [128, 256], F32)
    NSLOT = 3
    SLOTE = 8192
    wslot = nc.alloc_sbuf_tensor("wslot", [128, NSLOT, SLOTE], BF16)
    AW = 38400
    arena = nc.alloc_sbuf_tensor("arena", [128, AW], F32)
    pbank = [nc.alloc_psum_tensor("pb%d" % i, [128, 512], F32) for i in range(8)]
    Pb = [Buf("pb%d" % i) for i in range(8)]
    Bpv, Bcst, Bmod, Bgm, Bc1, Blb, Bscb, Bcbf, Bsmall = [Buf(n) for n in
                                                         ("pv", "cst", "mod", "gm", "c1", "lb", "scb", "cbf", "small")]

    def pvs(name, j0=0, n=None):
        o = PV_OFF[name] + j0
        if n is None:
            n = dict(PV_SEGS)[name] - j0
        return pv[:, o:o + n]

    def col(ap2d, j):
        return ap2d[:, j:j + 1]

    ident_f = cst[:, CST_OFF["ident"]:CST_OFF["ident"] + 128]
    ident_b = cbf[:, 0, :]
    trif_b = cbf[:, 1, :]
    trib_b = cbf[:, 2, :]
    ones_b = cbf[:, 3, :]

    def fap(ap2d, off, dims):
        return bass.AP(ap2d.tensor, ap2d.offset + off, [[ap2d.ap[0][0], ap2d.ap[0][1]]] + [list(d) for d in dims])

    class Arena:
        def __init__(self):
            self.off = 0

        def reset(self):
            self.off = 0

        def f32(self, n):
            a = arena[:, self.off:self.off + n]
            self.off += n
            assert self.off <= AW, ("arena overflow", self.off)
            return a

        def bf16(self, n):
            w = (n + 1) // 2
            a = arena[:, self.off:self.off + w].bitcast(BF16)
            self.off += w
            assert self.off <= AW, ("arena overflow", self.off)
            return a[:, 0:n]

    AR = Arena()

    def interleave(*gens):
        gens = list(gens)
        while gens:
            for g_ in list(gens):
                try:
                    next(g_)
                except StopIteration:
                    gens.remove(g_)

    class WS:
        def __init__(self):
            self.specs = []
            self.i = 0
            self.issued = 0
            self.bufs = [Buf("ws%d" % k) for k in range(NSLOT)]

        def _issue(self, idx):
            k = idx % NSLOT
            for (off, shape, src) in self.specs[idx]:
                n = int(np.prod(shape))
                dst = wslot[:, k, off:off + n]
                if len(shape) == 2:
                    dst = dst.rearrange("p (a b) -> p a b", a=shape[0])
                S.add("pool", lambda e, dst=dst, src=src: e.dma_start(out=dst, in_=src),
                      writes=[self.bufs[k]], dma=True)

        def next(self, spec, ahead=NSLOT - 1):
            if S.plan:
                self.specs.append(spec)
                return 0, None
            idx = self.i
            self.i += 1
            while self.issued < min(len(self.specs), idx + 1 + ahead):
                self._issue(self.issued)
                self.issued += 1
            return idx % NSLOT, self.bufs[idx % NSLOT]

    ws = WS()
    pbi = [0]

    def nextbank():
        k = pbi[0] % 7
        pbi[0] += 1
        return pbank[k], Pb[k]

    def wrows(wap, c0, ncols):
        return wap.rearrange("(kc p) n -> p kc n", p=128)[:, :, c0:c0 + ncols]

    def dbg_dump(name, src_d, B=None):
        if name in dbg_out:
            finals.append(S.add("sp", lambda e: e.dma_start(out=dbg_out[name], in_=src_d), dma=True))

    BLK_ALL = [(0, 256, 1), (256, 512, 0), (768, 512, 0), (1280, 512, 0), (1792, 512, 0)]
    BLK_LAT = BLK_ALL[1:]

    epsc = small[:, 255:256]
    onec = small[:, 254:255]
    Bsmall2 = Buf("constcols")

    def program():
        AR.reset()
        S.add("dve", lambda e: e.memset(epsc, EPS), writes=[Bsmall2])
        S.add("dve", lambda e: e.memset(onec, 1.0), writes=[Bsmall2])
        S.add("sp", lambda e: e.dma_start(out=pv[:], in_=pv_d), writes=[Bpv], dma=True)
        S.add("sp", lambda e: e.dma_start(out=cst[:], in_=cst_d[:, 0:NCST - (T + 4)]), writes=[Bcst], dma=True)
        for k, nm in enumerate(("ident", "trif", "trib", "ones")):
            S.add("dve", lambda e, k=k, nm=nm: e.tensor_copy(out=cbf[:, k, :],
                                                            in_=cst[:, CST_OFF[nm]:CST_OFF[nm] + 128]),
                  reads=[Bcst], writes=[Bcbf])
        S.add("act", lambda e: e.activation(out=scb[:, :, 0], in_=pvs("c"), func=AF.Silu), reads=[Bpv], writes=[Bscb])
        S.add("act", lambda e: e.activation(out=scb[:, :, 1], in_=pvs("cctx"), func=AF.Silu), reads=[Bpv], writes=[Bscb])
        sm = small
        S.add("act", lambda e: e.activation(out=sm[:, 0:32], in_=pvs("lam"), func=AF.Exp, scale=-1.0),
              reads=[Bpv], writes=[Bsmall])
        S.add("dve", lambda e: e.tensor_scalar(out=sm[:, 32:64], in0=sm[:, 0:32], scalar1=1.0, scalar2=None, op0=ALU.add),
              reads=[Bsmall], writes=[Bsmall])
        S.add("act", lambda e: e.activation(out=sm[:, 64:96], in_=sm[:, 32:64], func=AF.Ln), reads=[Bsmall], writes=[Bsmall])
        S.add("dve", lambda e: e.tensor_scalar(out=sm[:, 96:128], in0=sm[:, 32:64], scalar1=-1.0, scalar2=None, op0=ALU.add),
              reads=[Bsmall], writes=[Bsmall])
        S.add("dve", lambda e: e.reciprocal(out=sm[:, 96:128], in_=sm[:, 96:128]), reads=[Bsmall], writes=[Bsmall])
        S.add("dve", lambda e: e.tensor_tensor(out=sm[:, 96:128], in0=sm[:, 96:128], in1=sm[:, 0:32], op=ALU.mult),
              reads=[Bsmall], writes=[Bsmall])
        S.add("dve", lambda e: e.scalar_tensor_tensor(out=c1[:], in0=sm[:, 64:96], scalar=-8.0, in1=sm[:, 96:128],
                                                       op0=ALU.mult, op1=ALU.mult),
              reads=[Bsmall], writes=[Bc1])
        S.add("dve", lambda e: e.tensor_tensor(out=sm[:, 128:160], in0=pvs("hglb", 32, 32), in1=pvs("hglb", 0, 32),
                                               op=ALU.subtract), reads=[Bpv], writes=[Bsmall])
        S.add("act", lambda e: e.activation(out=lbv[:, 0, :], in_=sm[:, 128:160], func=AF.Sigmoid),
              reads=[Bsmall], writes=[Blb])
        S.add("dve", lambda e: e.tensor_scalar(out=lbv[:, 1, :], in0=lbv[:, 0, :], scalar1=-1.0, scalar2=1.0,
                                               op0=ALU.mult, op1=ALU.add), reads=[Blb], writes=[Blb])
        S.add("dve", lambda e: e.tensor_scalar(out=lbv[:, 2, :], in0=lbv[:, 0, :], scalar1=-1.0, scalar2=None,
                                               op0=ALU.add), reads=[Blb], writes=[Blb])

        def ada_gen(li, g0=0, g1=24):
            pt, pB = pbank[7], Pb[7]
            for g in range(g0, g1):
                k, wb = ws.next([(0, (16, 512), wrows(w_ada[li], g * 512, 512))])
                wv = wslot[:, k, 0:8192].rearrange("p (a b) -> p a b", a=16)
                for o4 in range(4):
                    oc = g * 4 + o4
                    for kc in range(16):
                        S.add("pe", lambda e, wv=wv, o4=o4, kc=kc, oc=oc, pt=pt: e.matmul(
                            pt[:, oc * 2:oc * 2 + 2], wv[:, kc, o4 * 128:(o4 + 1) * 128], scb[:, kc, :],
                            start=(kc == 0), stop=(kc == 15)),
                            reads=[wb, Bscb], writes=[pB])
                yield
            ca, cb = g0 * 4, g1 * 4
            badav = pvs("bada", li * 96 + ca, cb - ca)
            S.add("dve", lambda e, li=li, pt=pt, badav=badav: e.tensor_tensor(
                out=modv[:, li, ca:cb, :], in0=pt[:, 2 * ca:2 * cb].rearrange("p (a b) -> p a b", b=2),
                in1=fap(badav, 0, [[1, cb - ca], [0, 2]]), op=ALU.add), reads=[pB, Bpv], writes=[Bmod])
            for which, (gname, scbase) in enumerate((("gmix", 16), ("gffn", 64))):
                if not (ca <= scbase and scbase + 16 <= cb):
                    continue
                gv = pvs(gname, li * 16, 16)
                S.add("dve", lambda e, li=li, which=which, scbase=scbase, gv=gv: e.scalar_tensor_tensor(
                    out=gm[:, li, which, :, :], in0=modv[:, li, scbase:scbase + 16, :], scalar=1.0,
                    in1=fap(gv, 0, [[1, 16], [0, 2]]), op0=ALU.add, op1=ALU.mult),
                    reads=[Bmod, Bpv], writes=[Bgm])
            yield

        ada1 = itertools.chain(ada_gen(0, 8, 24), ada_gen(1))
        if "small" in dbg_out:
            interleave(ada_gen(0))
        if "small" in dbg_out:
            S.add("dve", lambda e: e.tensor_copy(out=sm[:, 160:192], in_=c1[:]), reads=[Bc1], writes=[Bsmall])
            S.add("dve", lambda e: e.tensor_copy(out=sm[:, 192:224], in_=lbv[:, 0, :]), reads=[Blb], writes=[Bsmall])
            dstg = AR.f32(1024)
            Bst = Buf()
            S.add("dve", lambda e: e.tensor_copy(out=dstg[:, 0:256], in_=sm[:]), reads=[Bsmall], writes=[Bst])
            S.add("dve", lambda e: e.tensor_copy(out=dstg[:, 256:640], in_=modv[:].rearrange("p a b c -> p (a b c)")),
                  reads=[Bmod], writes=[Bst])
            S.add("dve", lambda e: e.tensor_copy(out=dstg[:, 640:768], in_=gm[:].rearrange("p a b c d -> p (a b c d)")),
                  reads=[Bgm], writes=[Bst])
            S.add("dve", lambda e: e.memset(dstg[:, 768:1024], 0.0), writes=[Bst])
            finals.append(S.add("sp", lambda e: e.dma_start(out=dbg_out["small"], in_=dstg), reads=[Bst], dma=True))
            S.barrier()
        if stop_after == "ada":
            return

        AR.reset()
        xin = [AR.f32(2048) for _ in range(2)]
        Bxin = [Buf(), Buf()]
        stg = [AR.f32(16 * 512).rearrange("p (c t) -> p c t", c=16) for _ in range(2)]
        Bstg = [Buf(), Buf()]
        ti = 0
        for bi, (t0, n, isctx) in enumerate(BLK_ALL):
            sb = bi % 2
            for sub in range(n // 128):
                k = ti % 2
                ti += 1
                tok0 = t0 + sub * 128
                src = ctx_d[tok0:tok0 + 128, :] if isctx else x_d[tok0 - TC:tok0 - TC + 128, :]
                S.add("sp", lambda e, k=k, src=src: e.dma_start(out=xin[k], in_=src), writes=[Bxin[k]], dma=True)
                for q in range(4):
                    pt, pB = nextbank()
                    for j in range(4):
                        c = q * 4 + j
                        S.add("pe", lambda e, pt=pt, j=j, c=c, k=k: e.transpose(
                            out=pt[:, j * 128:(j + 1) * 128], in_=xin[k][:, c * 128:(c + 1) * 128], identity=ident_f),
                            reads=[Bxin[k], Bcst], writes=[pB])
                    eng = "act" if q % 2 == 0 else "dve"
                    dst = stg[sb][:, q * 4:q * 4 + 4, sub * 128:(sub + 1) * 128]
                    srcp = pt[:, :].rearrange("p (a b) -> p a b", a=4)
                    if eng == "act":
                        S.add("act", lambda e, dst=dst, srcp=srcp: e.activation(out=dst, in_=srcp, func=AF.Copy),
                              reads=[pB], writes=[Bstg[sb]])
                    else:
                        S.add("dve", lambda e, dst=dst, srcp=srcp: e.tensor_copy(out=dst, in_=srcp),
                              reads=[pB], writes=[Bstg[sb]])
            S.add("sp", lambda e, sb=sb, t0=t0, n=n: e.dma_start(
                out=xT_d[:, :, t0:t0 + n].rearrange("c p t -> p c t"), in_=stg[sb][:, :, 0:n]),
                reads=[Bstg[sb]], dma=True)
        if "small" not in dbg_out:
            interleave(ada_gen(0, 0, 8))
        S.barrier()
        dbg_dump("xT0", xT_d)
        if stop_after == "p0":
            return

        def norm_phase(li, which, blocks, hT, BhT, tokbase, colmajor=False, nbuf=2, bmax=512):
            interleave(norm_gen(li, which, blocks, hT, BhT, tokbase, nbuf=nbuf, bmax=bmax))

        def norm_gen(li, which, blocks, hT, BhT, tokbase, nbuf=2, bmax=512, dram_dest=None):
            if bmax < 512:
                blocks = [(t0 + o_, min(bmax, n - o_), ic) for (t0, n, ic) in blocks for o_ in range(0, n, bmax)]
            xb_ = [AR.f32(16 * bmax).rearrange("p (c t) -> p c t", c=16) for _ in range(nbuf)]
            Bx = [Buf() for _ in range(nbuf)]
            sq = AR.bf16(16 * bmax).rearrange("p (c t) -> p c t", c=16)
            Bsq = Buf()
            rs = AR.f32(bmax)
            Brs = Buf()
            tt = [AR.f32(bmax) for _ in range(2)]
            Bt = [Buf(), Buf()]
            if dram_dest is not None:
                stg_ = [AR.bf16(16 * bmax).rearrange("p (c t) -> p c t", c=16) for _ in range(2)]
                Bstg_ = [Buf(), Buf()]
            shbase = 0 if which == 0 else 48
            return _norm_inner(li, which, blocks, hT, BhT, tokbase, nbuf, dram_dest, xb_, Bx, sq, Bsq, rs, Brs, tt, Bt,
                               stg_ if dram_dest is not None else None, Bstg_ if dram_dest is not None else None, shbase)

        def _norm_inner(li, which, blocks, hT, BhT, tokbase, nbuf, dram_dest, xb_, Bx, sq, Bsq, rs, Brs, tt, Bt,
                        stg_, Bstg_, shbase):
            for bi, (t0, n, isctx) in enumerate(blocks):
                k = bi % nbuf
                S.add("sp", lambda e, k=k, t0=t0, n=n: e.dma_start(
                    out=xb_[k][:, :, 0:n], in_=xT_d[:, :, t0:t0 + n].rearrange("c p t -> p c t")),
                    writes=[Bx[k]], dma=True)
                S.add("act", lambda e, k=k, n=n: e.activation(out=sq[:, :, 0:n], in_=xb_[k][:, :, 0:n], func=AF.Square),
                      reads=[Bx[k]], writes=[Bsq])
                if dram_dest is not None:
                    yield
                pt, pB = nextbank()
                for c in range(16):
                    S.add("pe", lambda e, pt=pt, c=c, n=n: e.matmul(pt[:, 0:n], ones_b, sq[:, c, 0:n],
                                                                    start=(c == 0), stop=(c == 15)),
                          reads=[Bsq, Bcbf], writes=[pB])
                S.add("act", lambda e, pt=pt, n=n: e.activation(out=rs[:, 0:n], in_=pt[:, 0:n], func=AF.Sqrt,
                                                               scale=1.0 / D, bias=epsc),
                      reads=[pB, Bsmall2], writes=[Brs])
                S.add("dve", lambda e, n=n: e.reciprocal(out=rs[:, 0:n], in_=rs[:, 0:n]), reads=[Brs], writes=[Brs])
                p0 = t0 - tokbase
                sk = bi % 2
                for c in range(16):
                    kk = c % 2
                    S.add("dve", lambda e, k=k, c=c, kk=kk, n=n: e.tensor_tensor(
                        out=tt[kk][:, 0:n], in0=xb_[k][:, c, 0:n], in1=rs[:, 0:n], op=ALU.mult),
                        reads=[Bx[k], Brs], writes=[Bt[kk]])
                    if dram_dest is None:
                        dst, Bd = hT[:, c, p0:p0 + n], BhT
                    else:
                        dst, Bd = stg_[sk][:, c, 0:n], Bstg_[sk]
                    srcv = tt[kk][:, 0:n]
                    S.add("act", lambda e, dst=dst, srcv=srcv, c=c, isctx=isctx: e.activation(
                        out=dst, in_=srcv, func=AF.Identity,
                        scale=gm[:, li, which, c, isctx:isctx + 1], bias=modv[:, li, shbase + c, isctx:isctx + 1]),
                        reads=[Bt[kk], Bgm, Bmod], writes=[Bd])
                if dram_dest is not None:
                    S.add("sp", lambda e, sk=sk, p0=p0, n=n: e.dma_start(
                        out=dram_dest[:, :, p0:p0 + n].rearrange("c p t -> p c t"), in_=stg_[sk][:, :, 0:n]),
                        reads=[Bstg_[sk]], dma=True)
                yield

        def permute_colmajor(hT, BhT, tokbase):
            tmp = [AR.bf16(TL) for _ in range(2)]
            Bt = [Buf(), Buf()]
            for c in range(16):
                k = c % 2
                lat = hT[:, c, TC - tokbase:T - tokbase]
                S.add("dve", lambda e, k=k, lat=lat: e.tensor_copy(out=tmp[k], in_=lat), reads=[BhT], writes=[Bt[k]])
                S.add("dve", lambda e, k=k, lat=lat: e.tensor_copy(
                    out=lat.rearrange("p (w r) -> p w r", r=32), in_=fap(tmp[k], 0, [[1, 64], [64, 32]])),
                    reads=[Bt[k]], writes=[BhT])

        def linear_to_dram(hT, BhT, blocks, tokbase, wap, nout_chunks, dest, stage_dt):
            stgf = [AR.f32(T) for _ in range(2)]
            Bs = [Buf(), Buf()]
            for g in range(nout_chunks // 4):
                k, wb = ws.next([(0, (16, 512), wrows(wap, g * 512, 512))])
                wv = wslot[:, k, 0:8192].rearrange("p (a b) -> p a b", a=16)
                for o4 in range(4):
                    oc = g * 4 + o4
                    sk = oc % 2
                    dt_ = stage_dt(oc)
                    sview = stgf[sk] if dt_ == F32 else stgf[sk].bitcast(BF16)
                    for bi, (t0, n, isctx) in enumerate(blocks):
                        pt, pB = nextbank()
                        p0 = t0 - tokbase
                        for kc in range(16):
                            S.add("pe", lambda e, pt=pt, wv=wv, kc=kc, o4=o4, p0=p0, n=n: e.matmul(
                                pt[:, 0:n], wv[:, kc, o4 * 128:(o4 + 1) * 128], hT[:, kc, p0:p0 + n],
                                start=(kc == 0), stop=(kc == 15)), reads=[wb, BhT], writes=[pB])
                        if bi % 2 == 0:
                            S.add("act", lambda e, pt=pt, sview=sview, t0=t0, n=n: e.activation(
                                out=sview[:, t0:t0 + n], in_=pt[:, 0:n], func=AF.Copy), reads=[pB], writes=[Bs[sk]])
                        else:
                            S.add("dve", lambda e, pt=pt, sview=sview, t0=t0, n=n: e.tensor_copy(
                                out=sview[:, t0:t0 + n], in_=pt[:, 0:n]), reads=[pB], writes=[Bs[sk]])
                    ta, tb = blocks[0][0], blocks[-1][0] + blocks[-1][1]
                    S.add("sp", lambda e, oc=oc, sview=sview, ta=ta, tb=tb: e.dma_start(
                        out=dest(oc)[:, ta:tb], in_=sview[:, ta:tb]), reads=[Bs[sk]], dma=True)

        def out_proj_residual(li, wap, kchunks, act, Bact, blocks, tokbase, gate_chunk_base, slot_cols):
            interleave(_opr(li, wap, kchunks, act, Bact, blocks, tokbase, gate_chunk_base, slot_cols))

        def _opr(li, wap, kchunks, act, Bact, blocks, tokbase, gate_chunk_base, slot_cols):
            ta, tb = blocks[0][0], blocks[-1][0] + blocks[-1][1]
            xo = [AR.f32(tb - ta) for _ in range(2)]
            Bxo = [Buf(), Buf()]
            return _opr_inner(li, wap, kchunks, act, Bact, blocks, tokbase, gate_chunk_base, slot_cols, ta, tb, xo, Bxo)

        def _opr_inner(li, wap, kchunks, act, Bact, blocks, tokbase, gate_chunk_base, slot_cols, ta, tb, xo, Bxo):
            per = slot_cols // 128
            for g in range(16 // per):
                k, wb = ws.next([(0, (kchunks, slot_cols), wrows(wap, g * slot_cols, slot_cols))])
                wv = wslot[:, k, 0:kchunks * slot_cols].rearrange("p (a b) -> p a b", a=kchunks)
                for o4 in range(per):
                    oc = g * per + o4
                    sk = oc % 2
                    S.add("sp", lambda e, sk=sk, oc=oc: e.dma_start(out=xo[sk][:, 0:tb - ta], in_=xT_d[oc][:, ta:tb]),
                          writes=[Bxo[sk]], dma=True)
                    for bi, (t0, n, isctx) in enumerate(blocks):
                        pt, pB = nextbank()
                        p0 = t0 - tokbase
                        Ba_ = Bact[bi] if isinstance(Bact, list) else Bact
                        for kc in range(kchunks):
                            S.add("pe", lambda e, pt=pt, wv=wv, kc=kc, o4=o4, p0=p0, n=n: e.matmul(
                                pt[:, 0:n], wv[:, kc, o4 * 128:(o4 + 1) * 128], act[:, kc, p0:p0 + n],
                                start=(kc == 0), stop=(kc == kchunks - 1)), reads=[wb, Ba_], writes=[pB])
                        S.add("dve", lambda e, pt=pt, sk=sk, t0=t0, n=n, oc=oc, isctx=isctx: e.scalar_tensor_tensor(
                            out=xo[sk][:, t0 - ta:t0 - ta + n], in0=pt[:, 0:n],
                            scalar=modv[:, li, gate_chunk_base + oc, isctx:isctx + 1],
                            in1=xo[sk][:, t0 - ta:t0 - ta + n], op0=ALU.mult, op1=ALU.add),
                            reads=[pB, Bmod, Bxo[sk]], writes=[Bxo[sk]])
                    S.add("sp", lambda e, sk=sk, oc=oc: e.dma_start(out=xT_d[oc][:, ta:tb], in_=xo[sk][:, 0:tb - ta]),
                          reads=[Bxo[sk]], dma=True)
                    yield

        def ffn_phase(li, passes):
            for pi, blocks in enumerate(passes):
                nxt = passes[pi + 1] if pi + 1 < len(passes) else None
                ffn_pass(li, blocks, pi == 0, nxt)

        def ffn_pass(li, blocks, first, nxt_blocks):
            AR.reset()
            tokbase = blocks[0][0]
            tp = blocks[-1][0] + blocks[-1][1] - tokbase
            hT = AR.bf16(16 * tp).rearrange("p (c t) -> p c t", c=16)
            BhT = Buf()
            mark = AR.off
            if first:
                norm_phase(li, 1, blocks, hT, BhT, tokbase, bmax=256)
                S.barrier()
            else:
                S.add("sp", lambda e: e.dma_start(out=hT, in_=hT_d[:, :, 0:tp].rearrange("c p t -> p c t")),
                      writes=[BhT], dma=True)
            AR.off = mark
            A = AR.bf16(FC * tp).rearrange("p (c t) -> p c t", c=FC)
            BA = Buf()
            sg = [AR.f32(512) for _ in range(2)]
            Bsg = [Buf(), Buf()]
            cnt = 0
            for u in range(FC // 2):
                k, wb = ws.next([(0, (16, 256), wrows(w_ffn_in[li], u * 256, 256)),
                                 (4096, (16, 256), wrows(w_ffn_in[li], FF + u * 256, 256))])
                wg = wslot[:, k, 0:4096].rearrange("p (a b) -> p a b", a=16)
                wu = wslot[:, k, 4096:8192].rearrange("p (a b) -> p a b", a=16)
                for jj in range(2):
                    j = u * 2 + jj
                    for (t0, n, isctx) in blocks:
                        p0 = t0 - tokbase
                        pg, pgB = nextbank()
                        pu, puB = nextbank()
                        for (wv, pt, pB) in ((wg, pg, pgB), (wu, pu, puB)):
                            for kc in range(16):
                                S.add("pe", lambda e, pt=pt, wv=wv, kc=kc, jj=jj, p0=p0, n=n: e.matmul(
                                    pt[:, 0:n], wv[:, kc, jj * 128:(jj + 1) * 128], hT[:, kc, p0:p0 + n],
                                    start=(kc == 0), stop=(kc == 15)), reads=[wb, BhT], writes=[pB])
                        s_ = cnt % 2
                        cnt += 1
                        S.add("act", lambda e, pg=pg, s_=s_, n=n: e.activation(out=sg[s_][:, 0:n], in_=pg[:, 0:n],
                                                                              func=AF.Silu),
                              reads=[pgB], writes=[Bsg[s_]])
                        S.add("dve", lambda e, pu=pu, s_=s_, n=n, j=j, p0=p0: e.tensor_tensor(
                            out=A[:, j, p0:p0 + n], in0=sg[s_][:, 0:n], in1=pu[:, 0:n], op=ALU.mult),
                            reads=[Bsg[s_], puB], writes=[BA])
            gens = [_opr(li, w_ffn_out[li], FC, A, BA, blocks, tokbase, 80, 128)]
            if nxt_blocks is not None:
                S.barrier()
                mark2 = AR.off
                AR.off = 0
                ntb = nxt_blocks[0][0]
                gens.append(norm_gen(li, 1, nxt_blocks, None, None, ntb, nbuf=2, bmax=128, dram_dest=hT_d))
                assert AR.off <= mark, ("norm temps overflow hT region", AR.off, mark)
                AR.off = mark2
            interleave(*gens)
            S.barrier()

        def hgrn2_phase():
            li = 1
            AR.reset()
            hT = AR.bf16(16 * T).rearrange("p (c t) -> p c t", c=16)
            BhT = Buf()
            mark = AR.off
            norm_phase(1, 0, BLK_ALL, hT, BhT, 0, colmajor=True, nbuf=2, bmax=256)
            S.barrier()
            AR.off = mark
            permute_colmajor(hT, BhT, 0)
            S.add("sp", lambda e: e.dma_start(out=hT1_d.rearrange("c p t -> p c t"), in_=hT), reads=[BhT], dma=True)
            S.barrier()
            AR.reset()
            NCH = T // 128
            X1 = [AR.f32(T), AR.f32(T)]; X2 = [AR.f32(T), AR.f32(T)]; X3 = [AR.f32(T), AR.f32(T)]
            q = AR.f32(T)
            smask = AR.bf16(T + 4)
            v16 = AR.bf16(T)
            qe = [AR.bf16(T), AR.bf16(T)]; ke = [AR.bf16(T), AR.bf16(T)]
            vT = AR.bf16(NCH * 128).rearrange("p (c e) -> p c e", c=NCH)
            keT = [AR.bf16(NCH * 128).rearrange("p (c e) -> p c e", c=NCH) for _ in range(2)]
            SqA = [AR.bf16(NCH * 128).rearrange("p (c e) -> p c e", c=NCH) for _ in range(2)]
            att = [AR.bf16(128) for _ in range(4)]
            Rst = [[AR.f32(128), AR.f32(128)], [AR.f32(128), AR.f32(128)]]
            emb = AR.f32(2 * 2 * NCH).rearrange("p (d k c) -> p d k c", d=2, k=2)
            hTs = [AR.bf16(16 * 512).rearrange("p (c t) -> p c t", c=16) for _ in range(2)]
            BX1 = [Buf(), Buf()]; BX2 = [Buf(), Buf()]; BX3 = [Buf(), Buf()]
            Bq, Bmask, Bv16, BvT, Bemb = [Buf() for _ in range(5)]
            Bqe = [Buf(), Buf()]; Bke = [Buf(), Buf()]; BSq = [Buf(), Buf()]; BkeT = [Buf(), Buf()]
            Batt = [Buf() for _ in range(4)]
            BR = [[Buf(), Buf()], [Buf(), Buf()]]
            BhTs = [Buf(), Buf()]
            o = X3[1][:, 0:TL]; Bo = BX3[1]
            rsb = X3[0]; Brsb = BX3[0]
            g = X2[0]; Bg = BX2[0]
            ub = qe[1]; Bub = Bqe[1]
            ubp = ke[0]; Bubp = Bke[0]
            S.add("dve", lambda e: e.memset(smask, 1.0), writes=[Bmask])
            S.add("dve", lambda e: e.memset(fap(smask, 0, [[128, NCH + 1]]), 0.0), writes=[Bmask])
            ORD = [list(range(NCH)), [1, 0] + list(range(NCH - 1, 1, -1))]
            OFFS = [(63, 127), (64, 0)]
            SCALE = 128.0 ** -0.5
            c3 = lambda a: a.rearrange("p (c t) -> p c t", t=128)
            rot = [0]
            hcnt = [0]

            def lin_pass(p):
                doA = p < 16
                doG = p >= 1
                if doA:
                    kA, wbA = ws.next([(j * 2048, (16, 128), wrows(hg_w_in, col0 + p * 128, 128))
                                       for j, col0 in enumerate((2048, 4096, 0, 6144))], ahead=1)
                    wA = wslot[:, kA, 0:8192].rearrange("p (j a b) -> p j a b", j=4, a=16)
                if doG:
                    kG, wbG = ws.next([(0, (16, 128), wrows(hg_w_in, 8192 + (p - 1) * 128, 128))], ahead=1)
                    wG = wslot[:, kG, 0:2048].rearrange("p (a b) -> p a b", a=16)
                for bi, (t0, n, isctx) in enumerate(BLK_ALL):
                    hk = hcnt[0] % 2
                    hcnt[0] += 1
                    S.add("sp", lambda e, hk=hk, t0=t0, n=n: e.dma_start(
                        out=hTs[hk][:, :, 0:n], in_=hT1_d[:, :, t0:t0 + n].rearrange("c p t -> p c t")),
                        writes=[BhTs[hk]], dma=True)
                    yield
                    if doG and not isctx:
                        pt, pB = nextbank()
                        for kc in range(16):
                            S.add("pe", lambda e, pt=pt, kc=kc, hk=hk, n=n: e.matmul(
                                pt[:, 0:n], wG[:, kc, :], hTs[hk][:, kc, 0:n], start=(kc == 0), stop=(kc == 15)),
                                reads=[wbG, BhTs[hk]], writes=[pB])
                        S.add("act", lambda e, pt=pt, t0=t0, n=n: e.activation(out=g[:, t0:t0 + n], in_=pt[:, 0:n], func=AF.Copy),
                              reads=[pB], writes=[Bg])
                        yield
                    if doA:
                        for j in range(4):
                            pt, pB = nextbank()
                            for kc in range(16):
                                S.add("pe", lambda e, pt=pt, kc=kc, hk=hk, n=n, j=j: e.matmul(
                                    pt[:, 0:n], wA[:, j, kc, :], hTs[hk][:, kc, 0:n], start=(kc == 0), stop=(kc == 15)),
                                    reads=[wbA, BhTs[hk]], writes=[pB])
                            if j < 2:
                                S.add("act", lambda e, pt=pt, t0=t0, n=n, j=j: e.activation(
                                    out=X1[j][:, t0:t0 + n], in_=pt[:, 0:n], func=AF.Sigmoid), reads=[pB], writes=[BX1[j]])
                            elif j == 2:
                                S.add("dve", lambda e, pt=pt, t0=t0, n=n: e.tensor_copy(out=q[:, t0:t0 + n], in_=pt[:, 0:n]),
                                      reads=[pB], writes=[Bq])
                            else:
                                S.add("dve", lambda e, pt=pt, t0=t0, n=n: e.tensor_copy(out=v16[:, t0:t0 + n], in_=pt[:, 0:n]),
                                      reads=[pB], writes=[Bv16])
                            yield

            def vtrans():
                for c0 in range(0, NCH, 4):
                    pt, pB = nextbank()
                    ptb = pt[:, 0:256].bitcast(BF16).rearrange("p (a b) -> p a b", a=4)
                    nn = min(4, NCH - c0)
                    for j in range(nn):
                        S.add("pe", lambda e, ptb=ptb, j=j, c=c0 + j: e.transpose(
                            out=ptb[:, j, :], in_=v16[:, c * 128:(c + 1) * 128], identity=ident_b),
                            reads=[Bv16, Bcbf], writes=[pB])
                    S.add("act", lambda e, ptb=ptb, c0=c0, nn=nn: e.activation(
                        out=vT[:, c0:c0 + nn, :], in_=ptb[:, 0:nn, :], func=AF.Copy), reads=[pB], writes=[BvT])
                    yield

            def chain(n_, d):
                x1, x2, x3 = X1[d], X2[d], X3[d]
                B1, B2, B3 = BX1[d], BX2[d], BX3[d]
                lbc = lambda k_: lbv[:, k_, d * 16 + n_:d * 16 + n_ + 1]
                moff, loff = OFFS[d]
                S.add("act", lambda e: e.activation(out=x2, in_=x1, func=AF.Identity, scale=lbc(2), bias=lbc(1)),
                      reads=[B1, Blb], writes=[B2])
                yield
                S.add("act", lambda e: e.activation(out=x3, in_=x1, func=AF.Ln, scale=lbc(1), bias=lbc(0)),
                      reads=[B1, Blb], writes=[B3])
                yield
                if d == 0:
                    S.add("dve", lambda e: e.tensor_tensor_scan(out=x1, data0=smask[:, 0:T], data1=x3, initial=0.0,
                                                                op0=ALU.mult, op1=ALU.add),
                          reads=[B3, Bmask], writes=[B1])
                else:
                    rv = lambda a, o_=0: fap(a, T - 1 + o_, [[-1, T]])
                    S.add("dve", lambda e: e.tensor_tensor_scan(out=rv(x1), data0=rv(smask, 1), data1=rv(x3),
                                                                initial=0.0, op0=ALU.mult, op1=ALU.add),
                          reads=[B3, Bmask], writes=[B1])
                yield
                sv = lambda off, c0=0, cn=NCH: fap(x1, off + c0 * 128, [[128, cn]])
                dd = emb[:, d, 1, :]
                G = emb[:, d, 0, :]
                S.add("dve", lambda e: e.tensor_tensor(out=dd, in0=sv(loff), in1=sv(moff), op=ALU.subtract),
                      reads=[B1], writes=[Bemb])
                if d == 0:
                    S.add("dve", lambda e: e.tensor_tensor(out=G[:, 0:NCH - 1], in0=sv(moff, 1, NCH - 1), in1=dd[:, 0:NCH - 1],
                                                           op=ALU.add), reads=[B1, Bemb], writes=[Bemb])
                else:
                    S.add("dve", lambda e: e.tensor_tensor(out=G[:, 3:NCH], in0=sv(moff, 2, NCH - 3), in1=dd[:, 3:NCH],
                                                           op=ALU.add), reads=[B1, Bemb], writes=[Bemb])
                    S.add("dve", lambda e: e.tensor_tensor(out=G[:, 1:2], in0=sv(moff, 0, 1), in1=dd[:, 1:2], op=ALU.add),
                          reads=[B1, Bemb], writes=[Bemb])
                    S.add("dve", lambda e: e.tensor_tensor(out=G[:, 0:1], in0=sv(moff, NCH - 1, 1), in1=dd[:, 0:1], op=ALU.add),
                          reads=[B1, Bemb], writes=[Bemb])
                    S.add("dve", lambda e: e.memset(G[:, 2:3], 0.0), writes=[Bemb])
                if d == 0:
                    S.add("dve", lambda e: e.memset(G[:, NCH - 1:NCH], 0.0), writes=[Bemb])
                S.add("act", lambda e: e.activation(out=G, in_=G, func=AF.Exp), reads=[Bemb], writes=[Bemb])
                S.add("dve", lambda e: e.tensor_tensor(out=c3(x3), in0=c3(x1), in1=fap(x1, moff, [[128, NCH], [0, 128]]),
                                                       op=ALU.subtract), reads=[B1], writes=[B3])
                yield
                S.add("act", lambda e: e.activation(out=x1, in_=x3, func=AF.Exp), reads=[B3, Bemb], writes=[B1])
                yield
                S.add("act", lambda e: e.activation(out=x3, in_=x3, func=AF.Exp, scale=-1.0), reads=[B3], writes=[B3])
                yield
                S.add("dve", lambda e: e.scalar_tensor_tensor(out=qe[d], in0=q, scalar=SCALE, in1=x1,
                                                               op0=ALU.mult, op1=ALU.mult),
                      reads=[Bq, B1], writes=[Bqe[d]])
                yield
                S.add("dve", lambda e: e.tensor_tensor(out=ke[d], in0=x2, in1=x3, op=ALU.mult),
                      reads=[B2, B3], writes=[Bke[d]])
                yield
                for c0 in range(0, NCH, 4):
                    pt, pB = nextbank()
                    ptb = pt[:, 0:256].bitcast(BF16).rearrange("p (a b) -> p a b", a=4)
                    nn = min(4, NCH - c0)
                    for j in range(nn):
                        S.add("pe", lambda e, ptb=ptb, j=j, c=c0 + j: e.transpose(
                            out=ptb[:, j, :], in_=ke[d][:, c * 128:(c + 1) * 128], identity=ident_b),
                            reads=[Bke[d], Bcbf], writes=[pB])
                    S.add("act", lambda e, ptb=ptb, c0=c0, nn=nn: e.activation(
                        out=keT[d][:, c0:c0 + nn, :], in_=ptb[:, 0:nn, :], func=AF.Copy), reads=[pB], writes=[BkeT[d]])
                    yield

            def chunk_stage(n_):
                for s0 in range(0, NCH - 1, 4):
                    steps = list(range(s0, min(s0 + 4, NCH - 1)))
                    pds = []
                    for d in range(2):
                        pd, pdB = nextbank()
                        pds.append((pd, pdB))
                        for j, step in enumerate(steps):
                            c = ORD[d][step]
                            S.add("pe", lambda e, pd=pd, j=j, c=c, d=d: e.matmul(
                                pd[:, j * 128:(j + 1) * 128], keT[d][:, c, :], vT[:, c, :], start=True, stop=True),
                                reads=[BkeT[d], BvT], writes=[pdB])
                    yield
                    for j, step in enumerate(steps):
                        for d in range(2):
                            c = ORD[d][step]
                            cn = ORD[d][step + 1]
                            pd, pdB = pds[d]
                            cur, nxt = Rst[d][step % 2], Rst[d][(step + 1) % 2]
                            Bcur, Bnxt = BR[d][step % 2], BR[d][(step + 1) % 2]
                            if step == 0:
                                S.add("dve", lambda e, pd=pd, j=j, nxt=nxt: e.tensor_copy(out=nxt, in_=pd[:, j * 128:(j + 1) * 128]),
                                      reads=[pdB], writes=[Bnxt])
                            else:
                                cp = ORD[d][step - 1]
                                S.add("dve", lambda e, pd=pd, j=j, d=d, cp=cp, cur=cur, nxt=nxt: e.scalar_tensor_tensor(
                                    out=nxt, in0=cur, scalar=emb[:, d, 0, cp:cp + 1], in1=pd[:, j * 128:(j + 1) * 128],
                                    op0=ALU.mult, op1=ALU.add), reads=[Bcur, Bemb, pdB], writes=[Bnxt])
                            if cn >= 2:
                                S.add("pool", lambda e, d=d, c=c, cn=cn, nxt=nxt: e.tensor_scalar(
                                    out=SqA[d][:, cn, :], in0=nxt, scalar1=emb[:, d, 0, c:c + 1], scalar2=0.0,
                                    op0=ALU.mult, op1=ALU.add), reads=[Bnxt, Bemb], writes=[BSq[d]])
                        yield

                def att_stage(c):
                    ams = []
                    for d in range(2):
                        pa, paB = nextbank()
                        S.add("pe", lambda e, pa=pa, d=d: e.matmul(
                            pa[:, 0:128], ke[d][:, c * 128:(c + 1) * 128], qe[d][:, c * 128:(c + 1) * 128],
                            start=True, stop=True), reads=[Bke[d], Bqe[d]], writes=[paB])
                        r = rot[0] % 4
                        rot[0] += 1
                        msk = trif_b if d == 0 else trib_b
                        S.add("dve", lambda e, pa=pa, r=r, msk=msk: e.tensor_tensor(out=att[r], in0=pa[:, 0:128], in1=msk,
                                                                                    op=ALU.mult),
                              reads=[paB, Bcbf], writes=[Batt[r]])
                        ams.append(r)
                    return ams

                def o_stage(c, ams):
                    po, poB = nextbank()
                    for d in range(2):
                        r = ams[d]
                        S.add("pe", lambda e, po=po, r=r, d=d: e.matmul(po[:, 0:128], vT[:, c, :], att[r],
                                                                        start=(d == 0), stop=False),
                              reads=[BvT, Batt[r]], writes=[poB])
                        S.add("pe", lambda e, po=po, d=d: e.matmul(
                            po[:, 0:128], SqA[d][:, c, :], qe[d][:, c * 128:(c + 1) * 128], start=False, stop=(d == 1)),
                            reads=[BSq[d], Bqe[d]], writes=[poB])
                    S.add("act", lambda e, po=po: e.activation(out=o[:, (c - 2) * 128:(c - 1) * 128], in_=po[:, 0:128],
                                                               func=AF.Copy), reads=[poB], writes=[Bo])

                prev = att_stage(2)
                for c in range(2, NCH):
                    nxt_ = att_stage(c + 1) if c + 1 < NCH else None
                    o_stage(c, prev)
                    prev = nxt_
                    yield

            def tail(n_):
                sqb = qe[0]
                S.add("act", lambda e: e.activation(out=sqb[:, 0:TL], in_=o, func=AF.Square), reads=[Bo], writes=[Bqe[0]])
                for bi in range(4):
                    pt, pB = nextbank()
                    S.add("pe", lambda e, pt=pt, bi=bi: e.matmul(pt[:, 0:512], ones_b, sqb[:, bi * 512:(bi + 1) * 512],
                                                                 start=True, stop=True), reads=[Bqe[0], Bcbf], writes=[pB])
                    S.add("act", lambda e, pt=pt, bi=bi: e.activation(out=rsb[:, bi * 512:(bi + 1) * 512], in_=pt[:, 0:512],
                                                                      func=AF.Ln, scale=1.0 / 128, bias=epsc),
                          reads=[pB, Bsmall2], writes=[Brsb])
                S.add("act", lambda e: e.activation(out=rsb[:, 0:TL], in_=rsb[:, 0:TL], func=AF.Exp, scale=-0.5),
                      reads=[Brsb], writes=[Brsb])
                S.add("dve", lambda e: e.tensor_tensor(out=o, in0=o, in1=rsb[:, 0:TL], op=ALU.mult), reads=[Bo, Brsb], writes=[Bo])
                S.add("act", lambda e: e.activation(out=g[:, TC:T], in_=g[:, TC:T], func=AF.Silu), reads=[Bg], writes=[Bg])
                S.add("dve", lambda e: e.scalar_tensor_tensor(
                    out=ub[:, TC:T], in0=g[:, TC:T], scalar=pvs("hgnorm"), in1=o, op0=ALU.mult, op1=ALU.mult),
                    reads=[Bg, Bo, Bpv], writes=[Bub])
                S.add("pool", lambda e: e.tensor_copy(out=ubp[:, TC:T].rearrange("p (r w) -> p r w", w=64),
                                                      in_=fap(ub, TC, [[1, 32], [32, 64]])),
                      reads=[Bub], writes=[Bubp])
                S.add("sp", lambda e: e.dma_start(out=uT_d[n_][:, TC:T], in_=ubp[:, TC:T]), reads=[Bubp], dma=True)

            interleave(lin_pass(0))
            for n_ in range(16):
                c0_ = chain(n_, 0)
                for _ in range(2):
                    next(c0_)
                interleave(c0_, chain(n_, 1), vtrans())
                interleave(chunk_stage(n_), lin_pass(n_ + 1))
                tail(n_)
            S.barrier()
            dbg_dump("uT1", uT_d)
            if stop_after == "hgB":
                return
            AR.reset()
            U = AR.bf16(16 * TL).rearrange("p (c t) -> p c t", c=16)
            BU = [Buf() for _ in BLK_LAT]
            for bi_, (t0_, n_b, _) in enumerate(BLK_LAT):
                S.add("sp", lambda e, t0_=t0_, n_b=n_b: e.dma_start(
                    out=U[:, :, t0_ - TC:t0_ - TC + n_b], in_=uT_d[:, :, t0_:t0_ + n_b].rearrange("c p t -> p c t")),
                    writes=[BU[bi_]], dma=True)
            out_proj_residual(1, hg_w_out, 16, U, BU, BLK_LAT, TC, 32, 512)
            S.barrier()

        def final_phase():
            AR.reset()
            xb_ = [AR.f32(16 * 512).rearrange("p (c t) -> p c t", c=16) for _ in range(2)]
            Bx = [Buf(), Buf()]
            sq = [AR.bf16(16 * 512).rearrange("p (c t) -> p c t", c=16) for _ in range(2)]
            Bsq = [Buf(), Buf()]
            rs = [AR.f32(512), AR.f32(512)]
            Brs = [Buf(), Buf()]
            ot = [AR.f32(2048) for _ in range(2)]
            Bot = [Buf(), Buf()]
            oi = [0]

            def s1(bi):
                t0, n, _ = BLK_LAT[bi]
                k = bi % 2
                S.add("sp", lambda e: e.dma_start(
                    out=xb_[k][:, :, 0:n], in_=xT_d[:, :, t0:t0 + n].rearrange("c p t -> p c t")), writes=[Bx[k]], dma=True)
                S.add("act", lambda e: e.activation(out=sq[k], in_=xb_[k], func=AF.Square), reads=[Bx[k]], writes=[Bsq[k]])
                pt, pB = nextbank()
                for c in range(16):
                    S.add("pe", lambda e, pt=pt, c=c: e.matmul(pt[:, 0:512], ones_b, sq[k][:, c, :], start=(c == 0), stop=(c == 15)),
                          reads=[Bsq[k], Bcbf], writes=[pB])
                S.add("act", lambda e, pt=pt: e.activation(out=rs[k], in_=pt[:, 0:512], func=AF.Sqrt, scale=1.0 / D, bias=epsc),
                      reads=[pB, Bsmall2], writes=[Brs[k]])
                S.add("dve", lambda e: e.reciprocal(out=rs[k], in_=rs[k]), reads=[Brs[k]], writes=[Brs[k]])

            def s2(bi):
                k = bi % 2
                for c in range(16):
                    S.add("dve", lambda e, c=c: e.scalar_tensor_tensor(
                        out=xb_[k][:, c, :], in0=xb_[k][:, c, :], scalar=pvs("gfin", c, 1), in1=rs[k],
                        op0=ALU.mult, op1=ALU.mult), reads=[Bx[k], Brs[k], Bpv], writes=[Bx[k]])
                    if c % 4 == 3:
                        yield

            def s3(bi):
                t0, n, _ = BLK_LAT[bi]
                k = bi % 2
                for sub in range(4):
                    ok = oi[0] % 2
                    oi[0] += 1
                    for qd in range(4):
                        pt, pB = nextbank()
                        for j in range(4):
                            c = qd * 4 + j
                            S.add("pe", lambda e, pt=pt, j=j, c=c, sub=sub: e.transpose(
                                out=pt[:, j * 128:(j + 1) * 128], in_=xb_[k][:, c, sub * 128:(sub + 1) * 128], identity=ident_f),
                                reads=[Bx[k], Bcst], writes=[pB])
                        S.add("act", lambda e, pt=pt, ok=ok, qd=qd: e.activation(out=ot[ok][:, qd * 512:(qd + 1) * 512],
                                                                                 in_=pt[:, :], func=AF.Copy),
                              reads=[pB], writes=[Bot[ok]])
                    tok0 = t0 - TC + sub * 128
                    finals.append(S.add("sp", lambda e, ok=ok, tok0=tok0: e.dma_start(out=out_d[tok0:tok0 + 128, :], in_=ot[ok]),
                                        reads=[Bot[ok]], dma=True))
                    yield

            nb = len(BLK_LAT)
            s1(0)
            interleave(s2(0))
            s1(1)
            for bi in range(nb):
                gens = [s3(bi)]
                if bi + 1 < nb:
                    gens.append(s2(bi + 1))
                interleave(*gens)
                if bi + 2 < nb:
                    s1(bi + 2)


        AR.reset()
        hT = AR.bf16(16 * T).rearrange("p (c t) -> p c t", c=16)
        BhT = Buf()
        mark = AR.off
        norm_phase(0, 0, BLK_ALL, hT, BhT, 0, nbuf=2, bmax=256)
        S.barrier()
        AR.off = mark
        linear_to_dram(hT, BhT, BLK_ALL, 0, rg_w_in, 32, lambda oc: F_d[oc], lambda oc: F32)
        S.barrier()
        dbg_dump("F0", F_d)
        if stop_after == "rgA":
            return
        AR.reset()
        xb = [AR.f32(T), AR.f32(T)]; gg = [AR.f32(T), AR.f32(T), AR.f32(T)]
        xc = [AR.f32(T), AR.f32(T)]; gt = AR.f32(T)
        ra = [AR.f32(T), AR.f32(T)]; ig = [AR.f32(T), AR.f32(T)]; hd = [AR.f32(T), AR.f32(T)]
        xcb = [AR.bf16(T), AR.bf16(T)]; ub = AR.bf16(T)
        Bxb = [Buf(), Buf()]; Bgg = [Buf(), Buf(), Buf()]; Bxc = [Buf(), Buf()]; Bxcb = [Buf(), Buf()]
        Bgt, Bub = Buf(), Buf()
        Bra = [Buf(), Buf()]; Big = [Buf(), Buf()]; Bhd = [Buf(), Buf()]
        SEG = [(0, TC), (TC, T)]

        def rg_A(n_):
            par = n_ % 2
            xb_, gg_, xc_, xcb_ = xb[par], gg[n_ % 3], xc[par], xcb[par]
            Bxb_, Bgg_, Bxc_, Bxcb_ = Bxb[par], Bgg[n_ % 3], Bxc[par], Bxcb[par]
            cw = lambda kk: pvs("convw", kk * 16 + n_, 1)
            S.add("pool", lambda e: e.tensor_scalar(out=xc_, in0=xb_, scalar1=cw(2), scalar2=pvs("convb", n_, 1),
                                                    op0=ALU.mult, op1=ALU.add),
                  reads=[Bxb_, Bpv], writes=[Bxc_])
            yield
            for (s0, s1) in SEG:
                for (kk, sh) in ((1, 1), (0, 2), (3, -1)):
                    if sh > 0:
                        o_ = xc_[:, s0 + sh:s1]; i_ = xb_[:, s0:s1 - sh]
                    else:
                        o_ = xc_[:, s0:s1 + sh]; i_ = xb_[:, s0 - sh:s1]
                    S.add("dve", lambda e, o_=o_, i_=i_, kk=kk: e.scalar_tensor_tensor(
                        out=o_, in0=i_, scalar=cw(kk), in1=o_, op0=ALU.mult, op1=ALU.add),
                        reads=[Bxb_, Bxc_, Bpv], writes=[Bxc_])
                    yield
            S.add("pool", lambda e: e.tensor_copy(out=xcb_, in_=xc_), reads=[Bxc_], writes=[Bxcb_])
            yield
            S.add("act", lambda e: e.activation(out=gt, in_=gg_, func=AF.Square, scale=0.044715 ** 0.5),
                  reads=[Bgg_], writes=[Bgt])
            yield
            S.add("dve", lambda e: e.scalar_tensor_tensor(out=gt, in0=gt, scalar=1.0, in1=gg_, op0=ALU.add, op1=ALU.mult),
                  reads=[Bgt, Bgg_], writes=[Bgt])
            yield
            S.add("act", lambda e: e.activation(out=gt, in_=gt, func=AF.Sigmoid, scale=1.5957691216057308),
                  reads=[Bgt], writes=[Bgt])
            yield
            S.add("dve", lambda e: e.tensor_tensor(out=gg_, in0=gg_, in1=gt, op=ALU.mult), reads=[Bgt, Bgg_], writes=[Bgg_])
            yield

        def rg_BC(n_):
            par = n_ % 2
            xc_, xcb_, Bxc_, Bxcb_ = xc[par], xcb[par], Bxc[par], Bxcb[par]
            k, wb = ws.next([(0, (2, 128), rg_w_a[:, n_].rearrange("d p e -> p d e")),
                             (256, (2, 128), rg_w_i[:, n_].rearrange("d p e -> p d e"))])
            wga = wslot[:, k, 0:256].rearrange("p (a b) -> p a b", a=2)
            wgi = wslot[:, k, 256:512].rearrange("p (a b) -> p a b", a=2)
            for d in range(2):
                h = hd[d]; Bh = Bhd[d]; ra_ = ra[d]; ig_ = ig[d]; Bra_ = Bra[d]; Big_ = Big[d]
                for (wgt, dst, Bdst, bname) in ((wga, ra_, Bra_, "ba"), (wgi, ig_, Big_, "bi")):
                    for (t0, n, isctx) in BLK_ALL:
                        pt, pB = nextbank()
                        S.add("pe", lambda e, pt=pt, wgt=wgt, t0=t0, n=n, d=d: e.matmul(
                            pt[:, 0:n], wgt[:, d, :], xcb_[:, t0:t0 + n], start=True, stop=True),
                            reads=[wb, Bxcb_], writes=[pB])
                        S.add("act", lambda e, pt=pt, dst=dst, t0=t0, n=n, bname=bname, d=d: e.activation(
                            out=dst[:, t0:t0 + n], in_=pt[:, 0:n], func=AF.Sigmoid, bias=pvs(bname, d * 16 + n_, 1)),
                            reads=[pB, Bpv], writes=[Bdst])
                        yield
                S.add("act", lambda e, ra_=ra_, d=d: e.activation(out=ra_, in_=ra_, func=AF.Exp,
                                                                   scale=c1[:, d * 16 + n_:d * 16 + n_ + 1]),
                      reads=[Bra_, Bc1], writes=[Bra_])
                yield
                S.add("act", lambda e, h=h, ra_=ra_: e.activation(out=h, in_=ra_, func=AF.Square), reads=[Bra_], writes=[Bh])
                yield
                S.add("act", lambda e, h=h: e.activation(out=h, in_=h, func=AF.Sqrt, scale=-1.0, bias=onec),
                      reads=[Bh, Bsmall2], writes=[Bh])
                yield
                S.add("dve", lambda e, h=h, ig_=ig_: e.tensor_tensor(out=ig_, in0=ig_, in1=h, op=ALU.mult),
                      reads=[Big_, Bh], writes=[Big_])
                yield
                S.add("dve", lambda e, ig_=ig_: e.tensor_tensor(out=ig_, in0=ig_, in1=xc_, op=ALU.mult),
                      reads=[Big_, Bxc_], writes=[Big_])
                yield
                if d == 0:
                    S.add("dve", lambda e, h=h, ra_=ra_, ig_=ig_: e.tensor_tensor_scan(
                        out=h, data0=ra_, data1=ig_, initial=0.0, op0=ALU.mult, op1=ALU.add),
                        reads=[Bra_, Big_], writes=[Bh])
                else:
                    rv = lambda a, s0, s1: fap(a, s1 - 1, [[-1, s1 - s0]])
                    S.add("dve", lambda e, h=h, ra_=ra_, ig_=ig_, rv=rv: e.tensor_tensor_scan(
                        out=rv(h, 0, TC), data0=rv(ra_, 0, TC), data1=rv(ig_, 0, TC), initial=0.0,
                        op0=ALU.mult, op1=ALU.add), reads=[Bra_, Big_], writes=[Bh])
                    S.add("dve", lambda e, h=h, ra_=ra_, ig_=ig_, rv=rv: e.tensor_tensor_scan(
                        out=rv(h, TC, T), data0=rv(ra_, TC, T), data1=rv(ig_, TC, T), initial=h[:, 0:1],
                        op0=ALU.mult, op1=ALU.add), reads=[Bra_, Big_, Bh], writes=[Bh])
                yield

        def rg_loads(n_):
            S.add("sp", lambda e: e.dma_start(out=xb[n_ % 2], in_=F_d[n_]), writes=[Bxb[n_ % 2]], dma=True)
            S.add("sp", lambda e: e.dma_start(out=gg[n_ % 3], in_=F_d[16 + n_]), writes=[Bgg[n_ % 3]], dma=True)

        rg_loads(0)
        interleave(rg_A(0))
        rg_loads(1)
        for n_ in range(16):
            if n_ + 2 < 16:
                rg_loads(n_ + 2)
            gens = [rg_BC(n_)]
            if n_ + 1 < 16:
                gens.append(rg_A(n_ + 1))
            interleave(*gens)
            gg_ = gg[n_ % 3]
            S.add("dve", lambda e: e.tensor_tensor(out=hd[0], in0=hd[0], in1=hd[1], op=ALU.add),
                  reads=[Bhd[0], Bhd[1]], writes=[Bhd[0]])
            S.add("dve", lambda e, gg_=gg_: e.tensor_tensor(out=ub, in0=hd[0], in1=gg_, op=ALU.mult),
                  reads=[Bhd[0], Bgg[n_ % 3]], writes=[Bub])
            S.add("sp", lambda e, n_=n_: e.dma_start(out=uT_d[n_], in_=ub), reads=[Bub], dma=True)
            for _ in range(3):
                next(ada1, None)
        interleave(ada1)
        S.barrier()
        dbg_dump("uT0", uT_d)
        if stop_after == "rgB":
            return
        AR.reset()
        U = AR.bf16(16 * T).rearrange("p (c t) -> p c t", c=16)
        BU = [Buf() for _ in BLK_ALL]
        for bi_, (t0_, n_b, _) in enumerate(BLK_ALL):
            S.add("sp", lambda e, t0_=t0_, n_b=n_b: e.dma_start(
                out=U[:, :, t0_:t0_ + n_b], in_=uT_d[:, :, t0_:t0_ + n_b].rearrange("c p t -> p c t")),
                writes=[BU[bi_]], dma=True)
        out_proj_residual(0, rg_w_out, 16, U, BU, BLK_ALL, 0, 32, 512)
        S.barrier()
        dbg_dump("xT1", xT_d)
        if stop_after == "mix0":
            return
        ffn_phase(0, [[(0, 256, 1), (256, 512, 0), (768, 384, 0)], [(1152, 512, 0), (1664, 512, 0), (2176, 128, 0)]])
        dbg_dump("xT2", xT_d)
        if stop_after == "ffn0":
            return
        hgrn2_phase()
        dbg_dump("xT3", xT_d)
        if stop_after == "mix1":
            return
        ffn_phase(1, [[(256, 512, 0), (768, 512, 0)], [(1280, 512, 0), (1792, 512, 0)]])
        dbg_dump("xT4", xT_d)
        final_phase()

    S.plan = True
    try:
        program()
    except NotImplementedError:
        pass
    S.plan = False
    pbi[0] = 0
    try:
        program()
    except NotImplementedError:
        pass
    S.emit(final_waits=finals)
    return nc, list(dbg_out.keys())


def host_inputs(inputs, b):
    pvv = np.zeros((128, NPV), np.float32)

    def put(name, arr):
        a = _fm(arr)
        pvv[:, PV_OFF[name]:PV_OFF[name] + a.shape[1]] = a

    put("c", inputs["c"][b])
    put("cctx", inputs["c_ctx"])
    put("bada", inputs["b_ada"])
    put("gmix", inputs["g_mix"])
    put("gffn", inputs["g_ffn"])
    put("gfin", inputs["g_final"])
    put("convw", inputs["rg_conv_w"][0])
    put("convb", inputs["rg_conv_b"][0])
    put("ba", np.asarray(inputs["rg_b_a"][0]).reshape(2, 2048))
    put("bi", np.asarray(inputs["rg_b_i"][0]).reshape(2, 2048))
    put("lam", inputs["rg_lam"][0])
    put("hglb", inputs["hg_lb"])
    put("hgnorm", inputs["hg_norm"][0])
    f = lambda a: np.ascontiguousarray(np.asarray(a, np.float32))
    return {
        "x": f(inputs["x"][b]), "ctx": f(inputs["ctx"][b]), "pv": pvv, "cst": _consts(),
        "w_ada": f(inputs["w_ada"]), "w_ffn_in": f(inputs["w_ffn_in"]), "w_ffn_out": f(inputs["w_ffn_out"]),
        "rg_w_in": f(inputs["rg_w_in"][0]), "rg_w_a": f(inputs["rg_w_a"][0]), "rg_w_i": f(inputs["rg_w_i"][0]),
        "rg_w_out": f(inputs["rg_w_out"][0]), "hg_w_in": f(inputs["hg_w_in"][0]), "hg_w_out": f(inputs["hg_w_out"][0]),
    }


def kernel(**inputs):
    nc, _ = build()
    in_maps = [host_inputs(inputs, b) for b in range(8)]
    res = run_bass_kernel_spmd(nc, in_maps, core_ids=list(range(8)))
    return np.stack([np.asarray(r["out"], np.float32) for r in res.results], axis=0)
```
